# Optimizing a Trainium2 kernel written in Bass

```python
import math
import jax, jax.numpy as jnp
from jax import lax
import numpy as np

D_MODEL = 1024
BATCH = 8
SEQ = 4096
DEPTH = 2

CHUNK = 64
D_FF = 2816
MACARON_WEIGHT = 0.5
RMS_EPS = 1e-6
A_WIDTH = D_MODEL // 2
A_GROUPS = 8
CONV_WIDTH = 3
SB_HEADS = 8
SB_HEAD_DIM = (D_MODEL // 2) // SB_HEADS
B_WIDTH = SB_HEADS * SB_HEAD_DIM
SB_BLOCK = 128
AB_IN_COLS = 3 * A_WIDTH + 3 * B_WIDTH
C_WIDTH = D_MODEL
C_HEAD_DIM = 128
C_HEADS = C_WIDTH // C_HEAD_DIM
C_IN_COLS = 4 * C_WIDTH
N_EVEN = (DEPTH + 1) // 2
N_ODD = DEPTH // 2

kernel_name = "hybrid_shortconv_stickbreak_hgrn2_macaron"


def rms_norm(x, gain):
    x32 = x.astype(jnp.float32)
    y = x32 * lax.rsqrt(jnp.mean(x32 * x32, axis=-1, keepdims=True) + RMS_EPS)
    return (y * gain.astype(jnp.float32)).astype(x.dtype)


def swiglu(h, w_gate, w_up, w_down):
    return (jax.nn.silu(h @ w_gate) * (h @ w_up)) @ w_down


def stick_breaking_attention(q, k, v):
    bsz, nh, s_len, dh = q.shape
    nb = s_len // SB_BLOCK
    scale = 1.0 / math.sqrt(dh)
    qb = jnp.moveaxis(q.reshape(bsz, nh, nb, SB_BLOCK, dh), 2, 0)
    starts = jnp.arange(nb, dtype=jnp.int32) * SB_BLOCK
    kpos = jnp.arange(s_len, dtype=jnp.int32)

    def one_block(args):
        qi, start = args
        qpos = start + jnp.arange(SB_BLOCK, dtype=jnp.int32)
        mask = kpos[None, :] < qpos[:, None]
        z = jnp.einsum('bhqd,bhkd->bhqk', qi, k) * scale
        log_beta = jax.nn.log_sigmoid(z)
        log_keep = jnp.where(mask, jax.nn.log_sigmoid(-z), 0.0)
        later = lax.cumsum(log_keep, axis=3, reverse=True) - log_keep
        w = jnp.where(mask, jnp.exp(log_beta + later), 0.0)
        return jnp.einsum('bhqk,bhkd->bhqd', w, v)

    out = lax.map(one_block, (qb, starts))
    return jnp.moveaxis(out, 0, 2).reshape(bsz, nh, s_len, dh)


def shortconv_stickbreak_mixer(h, w_in, conv_w, w_out):
    bsz, s_len, _ = h.shape
    proj = h @ w_in
    a_b, a_c, a_x, q, k, v = jnp.split(proj, 6, axis=-1)
    u = a_c * a_x
    conv = lax.conv_general_dilated(
        u, conv_w[:, None, :].astype(u.dtype), window_strides=(1,),
        padding=[(CONV_WIDTH - 1, 0)], dimension_numbers=('NWC', 'WIO', 'NWC'),
        feature_group_count=A_WIDTH)
    y_a = a_b * conv
    def heads(t):
        return t.reshape(bsz, s_len, SB_HEADS, SB_HEAD_DIM).transpose(0, 2, 1, 3).astype(jnp.float32)
    y_b = stick_breaking_attention(heads(q), heads(k), heads(v))
    y_b = y_b.transpose(0, 2, 1, 3).reshape(bsz, s_len, B_WIDTH).astype(h.dtype)
    return jnp.concatenate([y_a, y_b], axis=-1) @ w_out


def chunkwise_gated_recurrence(q, log_f, k, v):
    bsz, nh, s_len, dk = q.shape
    dv = v.shape[-1]
    n_chunks = s_len // CHUNK

    def to_chunks(t):
        return jnp.moveaxis(t.reshape(bsz, nh, n_chunks, CHUNK, t.shape[-1]), 2, 0)

    tri = jnp.tril(jnp.ones((CHUNK, CHUNK), dtype=bool))

    def step(state, inp):
        qc, gc, kc, vc = inp
        b = jnp.cumsum(gc, axis=2)
        o_inter = jnp.einsum('bhtk,bhkv->bhtv', qc * jnp.exp(b), state)
        diff = b[:, :, :, None, :] - b[:, :, None, :, :]
        decay = jnp.exp(jnp.where(tri[None, None, :, :, None], diff, -jnp.inf))
        scores = jnp.einsum('bhtk,bhsk,bhtsk->bhts', qc, kc, decay)
        o_intra = jnp.einsum('bhts,bhsv->bhtv', scores, vc)
        b_last = b[:, :, -1:, :]
        new_state = (jnp.exp(b_last[:, :, 0, :])[..., None] * state
                     + jnp.einsum('bhsk,bhsv->bhkv', kc * jnp.exp(b_last - b), vc))
        return new_state, o_inter + o_intra

    state0 = jnp.zeros((bsz, nh, dk, dv), jnp.float32)
    _, out = lax.scan(step, state0, (to_chunks(q), to_chunks(log_f), to_chunks(k), to_chunks(v)))
    return jnp.moveaxis(out, 0, 2).reshape(bsz, nh, s_len, dv)


def hgrn2_mixer(h, w_in, lower_bound, out_norm, w_out):
    bsz, s_len, _ = h.shape
    proj = h @ w_in
    q, f, i, g = jnp.split(proj, 4, axis=-1)
    lb = lower_bound.astype(jnp.float32)
    log_f = jnp.logaddexp(jnp.log(lb), jnp.log1p(-lb) + jax.nn.log_sigmoid(f.astype(jnp.float32)))
    k = -jnp.expm1(log_f)
    q = jax.nn.silu(q.astype(jnp.float32))

    def heads(t):
        return t.reshape(bsz, s_len, C_HEADS, C_HEAD_DIM).transpose(0, 2, 1, 3).astype(jnp.float32)

    o = chunkwise_gated_recurrence(heads(q), heads(log_f), heads(k), heads(i))
    o = o * lax.rsqrt(jnp.mean(o * o, axis=-1, keepdims=True) + RMS_EPS) * out_norm.astype(jnp.float32)
    o = o.transpose(0, 2, 1, 3).reshape(bsz, s_len, C_WIDTH)
    o = (o * jax.nn.silu(g.astype(jnp.float32))).astype(h.dtype)
    return o @ w_out


def setup_inputs(seed: int = 0) -> dict:
    key = jax.random.key(seed)
    ks = jax.random.split(key, 20)
    f32 = jnp.float32

    def w(k, shape, fan_in):
        return jax.random.normal(k, shape, f32) * (fan_in ** -0.5)

    def gain(k, shape):
        return 1.0 + 0.02 * jax.random.normal(k, shape, f32)

    return {
        "x": jax.random.normal(ks[0], (BATCH, SEQ, D_MODEL), f32),
        "ffn_pre_norm": gain(ks[1], (DEPTH, D_MODEL)),
        "ffn_pre_w_gate": w(ks[2], (DEPTH, D_MODEL, D_FF), D_MODEL),
        "ffn_pre_w_up": w(ks[3], (DEPTH, D_MODEL, D_FF), D_MODEL),
        "ffn_pre_w_down": w(ks[4], (DEPTH, D_FF, D_MODEL), D_FF),
        "mix_norm": gain(ks[5], (DEPTH, D_MODEL)),
        "ffn_post_norm": gain(ks[6], (DEPTH, D_MODEL)),
        "ffn_post_w_gate": w(ks[7], (DEPTH, D_MODEL, D_FF), D_MODEL),
        "ffn_post_w_up": w(ks[8], (DEPTH, D_MODEL, D_FF), D_MODEL),
        "ffn_post_w_down": w(ks[9], (DEPTH, D_FF, D_MODEL), D_FF),
        "ab_w_in": w(ks[10], (N_EVEN, D_MODEL, AB_IN_COLS), D_MODEL),
        "ab_conv_w": w(ks[11], (N_EVEN, CONV_WIDTH, A_WIDTH), CONV_WIDTH),
        "ab_w_out": w(ks[12], (N_EVEN, A_WIDTH + B_WIDTH, D_MODEL), A_WIDTH + B_WIDTH),
        "c_w_in": w(ks[13], (N_ODD, D_MODEL, C_IN_COLS), D_MODEL),
        "c_lower_bounds": 0.1 * jax.random.normal(ks[14], (DEPTH, C_WIDTH), f32),
        "c_out_norm": gain(ks[15], (N_ODD, C_HEAD_DIM)),
        "c_w_out": w(ks[16], (N_ODD, C_WIDTH, D_MODEL), C_WIDTH),
        "final_norm": gain(ks[17], (D_MODEL,)),
    }


def reference(x, ffn_pre_norm, ffn_pre_w_gate, ffn_pre_w_up, ffn_pre_w_down, mix_norm,
              ffn_post_norm, ffn_post_w_gate, ffn_post_w_up, ffn_post_w_down,
              ab_w_in, ab_conv_w, ab_w_out, c_w_in, c_lower_bounds, c_out_norm, c_w_out,
              final_norm):
    lb_soft = jax.nn.softmax(c_lower_bounds.astype(jnp.float32), axis=0)
    lb_cum = jnp.cumsum(lb_soft, axis=0)
    lower_bounds = lb_cum - lb_cum[0:1]

    h = x
    for layer in range(DEPTH):
        h = h + MACARON_WEIGHT * swiglu(rms_norm(h, ffn_pre_norm[layer]), ffn_pre_w_gate[layer],
                                        ffn_pre_w_up[layer], ffn_pre_w_down[layer])
        hn = rms_norm(h, mix_norm[layer])
        if layer % 2 == 0:
            e = layer // 2
            h = h + shortconv_stickbreak_mixer(hn, ab_w_in[e], ab_conv_w[e], ab_w_out[e])
        else:
            o = layer // 2
            h = h + hgrn2_mixer(hn, c_w_in[o], lower_bounds[layer], c_out_norm[o], c_w_out[o])
        h = h + MACARON_WEIGHT * swiglu(rms_norm(h, ffn_post_norm[layer]), ffn_post_w_gate[layer],
                                        ffn_post_w_up[layer], ffn_post_w_down[layer])
    return rms_norm(h, final_norm)
```

```python
from contextlib import ExitStack
import numpy as np
import concourse.bass as bass
import concourse.mybir as mybir
from concourse.bass_utils import run_bass_kernel_spmd

F32 = mybir.dt.float32
BF16 = mybir.dt.bfloat16
AF = mybir.ActivationFunctionType
ALU = mybir.AluOpType

D = 1024
T = 4096
DFF = 2816
NFC = DFF // 128
EPS = 1e-6
COMPUTE = ("pe", "act", "dve", "pool")


class Buf:
    __slots__ = ("name", "w", "r")

    def __init__(self, name):
        self.name = name
        self.w = None
        self.r = []


class Sched:
    def __init__(self, nc, n_dma_sems=32, same_engine_sync=True):
        self.nc = nc
        self.eng = {"pe": nc.tensor, "act": nc.scalar, "dve": nc.vector,
                    "pool": nc.gpsimd, "sp": nc.sync}
        self.ops = {e: [] for e in self.eng}
        self.sems = {}
        self.cnt = {}
        for e in COMPUTE:
            self.sems[e] = nc.alloc_semaphore("s_" + e)
            self.cnt[e] = 0
        self.dma_sems = {"sp": [], "pool": []}
        for q, n in (("sp", n_dma_sems), ("pool", 16)):
            for i in range(n):
                k = "%s_d%d" % (q, i)
                self.sems[k] = nc.alloc_semaphore("s_" + k)
                self.cnt[k] = 0
                self.dma_sems[q].append(k)
        self.dma_rr = {"sp": 0, "pool": 0}
        self.seen = {e: {} for e in self.eng}
        self.same_engine_sync = same_engine_sync
        self.n_ops = 0

    def _deps(self, eng, reads, writes, extra=()):
        need = {}

        def add(d):
            if d is None:
                return
            k, v = d
            if k == eng and (eng == "pe" or not self.same_engine_sync):
                return
            if need.get(k, 0) < v:
                need[k] = v
        for b in reads:
            add(b.w)
        for b in writes:
            add(b.w)
            for d in b.r:
                add(d)
        for d in extra:
            add(d)
        seen = self.seen[eng]
        out = []
        for k, v in need.items():
            if seen.get(k, 0) < v:
                seen[k] = v
                out.append((k, v))
        return out

    def _mark(self, me, reads, writes):
        for b in reads:
            b.r.append(me)
            if len(b.r) > 64:
                b.r = b.r[-48:] if False else b.r
        for b in writes:
            b.w = me
            b.r = []

    def op(self, eng, fn, reads=(), writes=()):
        waits = self._deps(eng, reads, writes)
        self.cnt[eng] += 1
        me = (eng, self.cnt[eng])
        self._mark(me, reads, writes)
        self.ops[eng].append((waits, fn, (eng, 1)))
        self.n_ops += 1
        return me

    def dma(self, eng, fn, reads=(), writes=()):
        pool = self.dma_sems[eng]
        k = pool[self.dma_rr[eng] % len(pool)]
        self.dma_rr[eng] += 1
        extra = [(k, self.cnt[k])] if self.cnt[k] else []
        waits = self._deps(eng, reads, writes, extra)
        self.cnt[k] += 16
        me = (k, self.cnt[k])
        self._mark(me, reads, writes)
        self.ops[eng].append((waits, fn, (k, 16)))
        self.n_ops += 1
        return me

    def wait_all(self, eng, bufs):
        waits = self._deps(eng, bufs, bufs)
        self.ops[eng].append((waits, None, None))

    def emit(self):
        nc = self.nc
        sems = self.sems
        ops = self.ops
        with nc.Block() as block:
            def run(engname, e):
                for waits, fn, inc in ops[engname]:
                    for k, v in waits:
                        e.wait_ge(sems[k], v)
                    if fn is not None:
                        fn(e).then_inc(sems[inc[0]], inc[1])

            @block.tensor
            def _(e):
                run("pe", e)

            @block.scalar
            def _(e):
                run("act", e)

            @block.vector
            def _(e):
                run("dve", e)

            @block.gpsimd
            def _(e):
                run("pool", e)

            @block.sync
            def _(e):
                run("sp", e)
        self.ops = {e: [] for e in self.eng}


def run_zip(gens):
    gens = [g for g in gens if g is not None]
    while gens:
        for g in list(gens):
            try:
                next(g)
            except StopIteration:
                gens.remove(g)


class Ctx:
    N = [0]

    def __init__(self, nc, S, es):
        self.nc, self.S, self.es = nc, S, es

    def sb(self, name, shape, dt):
        Ctx.N[0] += 1
        t = self.es.enter_context(self.nc.sbuf_tensor("%s_%d" % (name, Ctx.N[0]), shape, dt))
        return t, Buf(name)

    def ps(self, name):
        Ctx.N[0] += 1
        t = self.es.enter_context(self.nc.psum_tensor("%s_%d" % (name, Ctx.N[0]), [128, 512], F32))
        return t, Buf(name)


def load_consts(c, S, dr):
    ident, b_ident = c.sb("ident", [128, 128], F32)
    ones, b_ones = c.sb("ones", [128, 128], BF16)
    epsb, b_eps = c.sb("epsb", [128, 1], F32)
    S.dma("sp", lambda e: e.dma_start(out=ident[:, :], in_=dr["c_ident"][:, :]), writes=[b_ident])
    S.op("dve", lambda e: e.memset(ones[:, :], 1.0), writes=[b_ones])
    S.op("dve", lambda e: e.memset(epsb[:, :], EPS), writes=[b_eps])
    return (ident, b_ident), (ones, b_ones), (epsb, b_eps)


def ffn_phase(nc, S, dr, layer, which, src, b_src, dst, b_dst, first=False, last=False, ntiles=None):
    NT = 256
    n_tiles = T // NT if ntiles is None else ntiles
    wg_d = dr["ffn_%s_w_gate" % which][layer]
    wu_d = dr["ffn_%s_w_up" % which][layer]
    wd_d = dr["ffn_%s_w_down" % which][layer]
    gain_d = dr["ffn_%s_norm" % which][layer]
    with ExitStack() as es:
        c = Ctx(nc, S, es)
        (ident, b_ident), (ones, b_ones), (epsb, b_eps) = load_consts(c, S, dr)
        wg = [c.sb("wg", [128, 8, DFF // 2], BF16) for _ in range(2)]
        wu = [c.sb("wu", [128, 8, DFF // 2], BF16) for _ in range(2)]
        wd = [c.sb("wd", [128, NFC // 2, D], BF16) for _ in range(2)]
        gain, b_gain = c.sb("gain", [128, 8], F32)
        hT = [c.sb("hT", [128, 8, NT], F32) for _ in range(2)]
        sq, b_sq = c.sb("sq", [128, 8, NT], BF16)
        hn, b_hn = c.sb("hn", [128, 8, NT], BF16)
        srt, b_srt = c.sb("srt", [128, NT], F32)
        rstd, b_rstd = c.sb("rstd", [128, NT], F32)
        act, b_act = c.sb("act", [128, NFC, NT], BF16)
        sg = [c.sb("sg", [128, NT], F32) for _ in range(2)]
        pg = [c.ps("pg") for _ in range(2)]
        pu = [c.ps("pu") for _ in range(2)]
        pd = [c.ps("pd") for _ in range(2)]
        pss, b_pss = c.ps("pss")
        if first or last:
            pT, b_pT = c.ps("pT")
            xs = [c.sb("xs", [128, D], F32) for _ in range(2)]
        if last:
            gfb, b_gfb = c.sb("gfb", [128, D], F32)
            ss1, b_ss1 = c.sb("ss1", [128, 1], F32)
            ss2, b_ss2 = c.sb("ss2", [128, 1], F32)
            junk, b_junk = c.sb("junk", [128, D], BF16)
            S.dma("sp", lambda e: e.dma_start(out=gfb[:, :], in_=dr["final_norm"].partition_broadcast(128)),
                  writes=[b_gfb])

        H = DFF // 2
        for half in range(2):
            for (wt, wsrc) in ((wg, wg_d), (wu, wu_d)):
                t_, b_ = wt[half]
                S.dma("pool", lambda e, t_=t_, wsrc=wsrc, half=half: e.dma_start(
                    out=t_[:, :, :],
                    in_=wsrc[:, half * H:(half + 1) * H].rearrange("(c p) f -> p c f", p=128),
                    max_dma_last_dim=H * 4), writes=[b_])
        for half in range(2):
            t_, b_ = wd[half]
            S.dma("pool", lambda e, t_=t_, half=half: e.dma_start(
                out=t_[:, :, :],
                in_=wd_d[half * (DFF // 2):(half + 1) * (DFF // 2), :].rearrange("(c p) d -> p c d", p=128),
                max_dma_last_dim=4096), writes=[b_])
        S.dma("sp", lambda e: e.dma_start(out=gain[:, :], in_=gain_d.rearrange("(c p) -> p c", p=128),
                                          allow_slow_non_contiguous=True), writes=[b_gain])

        def load_norm(i):
            b = i % 2
            h_, bh = hT[b]
            t0 = i * NT
            if first:
                for j in range(2):
                    x_, bx = xs[j]
                    S.dma("sp", lambda e, x_=x_, j=j: e.dma_start(
                        out=x_[:, :], in_=src[t0 + j * 128: t0 + (j + 1) * 128, :]), reads=[b_src], writes=[bx])
                    for g4 in range(2):
                        for k in range(4):
                            cc = g4 * 4 + k
                            S.op("pe", lambda e, x_=x_, cc=cc, k=k: e.transpose(
                                pT[:, k * 128:(k + 1) * 128], x_[:, cc * 128:(cc + 1) * 128], ident[:, :]),
                                reads=[bx, b_ident], writes=[b_pT])
                        S.op("act", lambda e, h_=h_, g4=g4, j=j: e.copy(
                            h_[:, g4 * 4:(g4 + 1) * 4, j * 128:(j + 1) * 128],
                            pT[:, :].rearrange("p (k t) -> p k t", k=4)), reads=[b_pT], writes=[bh])
            else:
                S.dma("sp", lambda e, h_=h_: e.dma_start(out=h_[:, :, :], in_=src[:, :, t0:t0 + NT].rearrange("c p t -> p c t")),
                      reads=[b_src], writes=[bh])
            S.op("act", lambda e, h_=h_: e.activation(sq[:, :, :], h_[:, :, :], AF.Square), reads=[bh], writes=[b_sq])
            for cc in range(8):
                S.op("pe", lambda e, cc=cc: e.matmul(pss[:, :NT], ones[:, :], sq[:, cc, :], start=(cc == 0), stop=(cc == 7)),
                     reads=[b_ones, b_sq], writes=[b_pss])
            S.op("act", lambda e: e.activation(srt[:, :], pss[:, :NT], AF.Sqrt, bias=epsb[:, 0:1], scale=1.0 / D),
                 reads=[b_pss, b_eps], writes=[b_srt])
            S.op("dve", lambda e: e.reciprocal(rstd[:, :], srt[:, :]), reads=[b_srt], writes=[b_rstd])
            for cc in range(8):
                S.op("dve", lambda e, h_=h_, cc=cc: e.scalar_tensor_tensor(
                    hn[:, cc, :], h_[:, cc, :], gain[:, cc:cc + 1], rstd[:, :], ALU.mult, ALU.mult),
                    reads=[bh, b_gain, b_rstd], writes=[b_hn])

        def gate_up(i):
            for f in range(NFC):
                half, fl = divmod(f, NFC // 2)
                pg_, bpg = pg[f % 2]
                pu_, bpu = pu[f % 2]
                sg_, bsg = sg[f % 2]
                for (wt, p_, bp) in ((wg, pg_, bpg), (wu, pu_, bpu)):
                    w_, bw = wt[half]
                    for cc in range(8):
                        S.op("pe", lambda e, w_=w_, p_=p_, cc=cc, fl=fl: e.matmul(
                            p_[:, :NT], w_[:, cc, fl * 128:(fl + 1) * 128], hn[:, cc, :],
                            start=(cc == 0), stop=(cc == 7)), reads=[bw, b_hn], writes=[bp])
                S.op("act", lambda e, sg_=sg_, pg_=pg_: e.activation(sg_[:, :], pg_[:, :NT], AF.Silu),
                     reads=[bpg], writes=[bsg])
                S.op("dve", lambda e, sg_=sg_, pu_=pu_, f=f: e.tensor_tensor(
                    act[:, f, :], sg_[:, :], pu_[:, :NT], ALU.mult), reads=[bsg, bpu], writes=[b_act])

        def down_store(i):
            b = i % 2
            h_, bh = hT[b]
            t0 = i * NT
            for d in range(8):
                pd_, bpd = pd[d % 2]
                for f in range(NFC):
                    half, fl = divmod(f, NFC // 2)
                    w_, bw = wd[half]
                    S.op("pe", lambda e, w_=w_, pd_=pd_, fl=fl, d=d, f=f: e.matmul(
                        pd_[:, :NT], w_[:, fl, d * 128:(d + 1) * 128], act[:, f, :],
                        start=(f == 0), stop=(f == NFC - 1)), reads=[bw, b_act], writes=[bpd])
                S.op("dve", lambda e, h_=h_, pd_=pd_, d=d: e.scalar_tensor_tensor(
                    h_[:, d, :], pd_[:, :NT], 0.5, h_[:, d, :], ALU.mult, ALU.add),
                    reads=[bpd, bh], writes=[bh])
            if not last:
                S.dma("sp", lambda e, h_=h_: e.dma_start(out=dst[:, :, t0:t0 + NT].rearrange("c p t -> p c t"), in_=h_[:, :, :]),
                      reads=[bh], writes=[b_dst])
            else:
                for j in range(2):
                    x_, bx = xs[j]
                    for g4 in range(2):
                        for k in range(4):
                            cc = g4 * 4 + k
                            S.op("pe", lambda e, h_=h_, cc=cc, k=k, j=j: e.transpose(
                                pT[:, k * 128:(k + 1) * 128], h_[:, cc, j * 128:(j + 1) * 128], ident[:, :]),
                                reads=[bh, b_ident], writes=[b_pT])
                        S.op("act", lambda e, x_=x_, g4=g4: e.copy(x_[:, g4 * 512:(g4 + 1) * 512], pT[:, :]),
                             reads=[b_pT], writes=[bx])
                    S.op("act", lambda e, x_=x_: e.activation(junk[:, :], x_[:, :], AF.Square, accum_out=ss1[:, 0:1]),
                         reads=[bx], writes=[b_junk, b_ss1])
                    S.op("act", lambda e: e.activation(ss2[:, :], ss1[:, :], AF.Sqrt, bias=epsb[:, 0:1], scale=1.0 / D),
                         reads=[b_ss1, b_eps], writes=[b_ss2])
                    S.op("dve", lambda e: e.reciprocal(ss1[:, :], ss2[:, :]), reads=[b_ss2], writes=[b_ss1])
                    S.op("dve", lambda e, x_=x_: e.scalar_tensor_tensor(
                        x_[:, :], x_[:, :], ss1[:, 0:1], gfb[:, :], ALU.mult, ALU.mult),
                        reads=[bx, b_ss1, b_gfb], writes=[bx])
                    S.dma("sp", lambda e, x_=x_, j=j: e.dma_start(
                        out=dst[t0 + j * 128: t0 + (j + 1) * 128, :], in_=x_[:, :]), reads=[bx], writes=[b_dst])

        load_norm(0)
        for i in range(n_tiles):
            gate_up(i)
            if i + 1 < n_tiles:
                load_norm(i + 1)
            down_store(i)
        S.wait_all("sp", [b_dst])
        S.emit()


def norm_tile(S, h_, bh, hn, b_hn, sq, b_sq, pss, b_pss, srt, b_srt, rstd, b_rstd, gain, b_gain,
              ones, b_ones, epsb, b_eps, NT):
    S.op("act", lambda e: e.activation(sq[:, :, :], h_[:, :, :], AF.Square), reads=[bh], writes=[b_sq])
    for cc in range(8):
        S.op("pe", lambda e, cc=cc: e.matmul(pss[:, :NT], ones[:, :], sq[:, cc, :], start=(cc == 0), stop=(cc == 7)),
             reads=[b_ones, b_sq], writes=[b_pss])
    S.op("act", lambda e: e.activation(srt[:, :], pss[:, :NT], AF.Sqrt, bias=epsb[:, 0:1], scale=1.0 / D),
         reads=[b_pss, b_eps], writes=[b_srt])
    S.op("dve", lambda e: e.reciprocal(rstd[:, :], srt[:, :]), reads=[b_srt], writes=[b_rstd])
    for cc in range(8):
        S.op("dve", lambda e, cc=cc: e.scalar_tensor_tensor(
            hn[:, cc, :], h_[:, cc, :], gain[:, cc:cc + 1], rstd[:, :], ALU.mult, ALU.mult),
            reads=[bh, b_gain, b_rstd], writes=[b_hn])


def ab_phase(nc, S, dr, src, b_src, dst, b_dst, ntiles=None):
    NT = 256
    n_tiles = T // NT if ntiles is None else ntiles
    w_in_d = dr["ab_w_in"][0]
    w_out_d = dr["ab_w_out"][0]
    with ExitStack() as es:
        c = Ctx(nc, S, es)
        (ident, b_ident), (ones, b_ones), (epsb, b_eps) = load_consts(c, S, dr)
        onec, b_onec = c.sb("onec", [128, 1], F32)
        S.op("dve", lambda e: e.memset(onec[:, :], 1.0), writes=[b_onec])
        win = [c.sb("win", [128, 8, 512], BF16) for _ in range(6)]
        wout, b_wout = c.sb("wout", [128, 8, D], BF16)
        cw, b_cw = c.sb("cw", [128, 3, 4], F32)
        gain, b_gain = c.sb("gain", [128, 8], F32)
        mU, b_mU = c.sb("mU", [128, 128], BF16)
        mL, b_mL = c.sb("mL", [128, 128], BF16)
        mlo, b_mlo = c.sb("mlo", [128, 256], F32)
        mhi, b_mhi = c.sb("mhi", [128, 256], F32)
        kT, b_kT = c.sb("kT", [128, 4, T], BF16)
        vt, b_vt = c.sb("vt", [128, T // 128, 512], BF16)
        hTs = [c.sb("hT", [128, 8, NT], F32) for _ in range(2)]
        hn, b_hn = c.sb("hn", [128, 8, NT], BF16)
        sq, b_sq = hn, b_hn
        srt, b_srt = c.sb("srt", [128, NT], F32)
        rstd, b_rstd = c.sb("rstd", [128, NT], F32)
        ab, b_ab = c.sb("ab", [128, 4, NT], F32)
        u, b_u = c.sb("u", [128, 4, NT + 2], F32)
        t1, b_t1 = c.sb("t1", [128, NT], F32)
        qTs = [c.sb("qT", [128, 4, NT], BF16) for _ in range(2)]
        yTs = [c.sb("yT", [128, 8, NT], BF16) for _ in range(2)]
        b_kTs = [Buf("kT%d" % i) for i in range(T // NT)]
        b_vts = [Buf("vt%d" % i) for i in range(T // NT)]
        eb = [c.sb("eb", [128, 1024], F32) for _ in range(3)]
        spb = [c.sb("spb", [128, 1024], BF16) for _ in range(3)]
        gb = [c.sb("gb", [128, 1024], F32) for _ in range(2)]
        wb = [c.sb("wb", [128, 1024], BF16) for _ in range(2)]
        PP = []
        for i in range(4):
            Ctx.N[0] += 1
            t_ = es.enter_context(nc.psum_tensor("pp_%d" % Ctx.N[0], [128, 1024], F32))
            PP.append((t_, [Buf("pp%d_0" % i), Buf("pp%d_1" % i)]))
        banks = []
        for b in range(2):
            banks.append((PP[2][0], b * 512, PP[2][1][b]))
        rr = [0]

        def next_bank():
            rr[0] += 1
            return banks[rr[0] % len(banks)]

        for g in range(6):
            t_, b_ = win[g]
            S.dma("pool", lambda e, t_=t_, g=g: e.dma_start(
                out=t_[:, :, :], in_=w_in_d[:, g * 512:(g + 1) * 512].rearrange("(c p) f -> p c f", p=128)),
                writes=[b_])
        S.dma("pool", lambda e: e.dma_start(out=wout[:, :, :], in_=w_out_d.rearrange("(c p) d -> p c d", p=128)),
              writes=[b_wout])
        S.dma("pool", lambda e: e.dma_start(out=mU[:, :], in_=dr["c_U"][:, :]), writes=[b_mU])
        S.dma("pool", lambda e: e.dma_start(out=mL[:, :], in_=dr["c_L"][:, :]), writes=[b_mL])
        S.dma("sp", lambda e: e.dma_start(out=mlo[:, :], in_=dr["c_mlo"][:, :]), writes=[b_mlo])
        S.dma("sp", lambda e: e.dma_start(out=mhi[:, :], in_=dr["c_mhi"][:, :]), writes=[b_mhi])
        S.dma("sp", lambda e: e.dma_start(out=gain[:, :], in_=dr["mix_norm"][0].rearrange("(c p) -> p c", p=128),
                                          allow_slow_non_contiguous=True), writes=[b_gain])
        S.dma("sp", lambda e: e.dma_start(out=cw[:, :, :], in_=dr["ab_conv_w"][0].rearrange("j (c p) -> p j c", p=128),
                                          allow_slow_non_contiguous=True), writes=[b_cw])
        S.op("pool", lambda e: e.memset(u[:, :, 0:2], 0.0), writes=[b_u])

        pend = []

        def flush():
            for f in pend:
                f()
            del pend[:]

        def proj_fm(g, cc, evac):
            flush()
            t_, off, bb = next_bank()
            w_, bw = win[g]
            for kc in range(8):
                S.op("pe", lambda e, t_=t_, off=off, w_=w_, kc=kc, cc=cc: e.matmul(
                    t_[:, off:off + NT], w_[:, kc, cc * 128:(cc + 1) * 128], hn[:, kc, :],
                    start=(kc == 0), stop=(kc == 7)), reads=[bw, b_hn], writes=[bb])
            pend.append(lambda: evac(t_[:, off:off + NT], bb))

        qkv_ready = [False] * n_tiles
        done1 = [False] * n_tiles
        done2 = [False] * n_tiles
        b_yas = [Buf("ya0"), Buf("ya1")]

        def stage1(i):
            while i >= 2 and not done2[i - 2]:
                yield
            t0 = i * NT
            hT, bh = hTs[i % 2]
            qT, b_qT = qTs[i % 2]
            yT, b_yT = yTs[i % 2]
            b_ya = b_yas[i % 2]
            b_kT = b_kTs[i]
            b_vt = b_vts[i]
            S.dma("sp", lambda e, t0=t0: e.dma_start(out=hT[:, :, :], in_=src[:, :, t0:t0 + NT].rearrange("c p t -> p c t")),
                  reads=[b_src], writes=[bh])
            pss_t, pss_off, b_pss = next_bank()
            norm_tile(S, hT, bh, hn, b_hn, sq, b_sq, pss_t[:, pss_off:pss_off + 512], b_pss, srt, b_srt, rstd, b_rstd,
                      gain, b_gain, ones, b_ones, epsb, b_eps, NT)
            yield
            for cc in range(4):
                proj_fm(3, cc, lambda p, bb, cc=cc: S.op("dve", lambda e: e.tensor_scalar(qT[:, cc, :], p, 0.125, None, ALU.mult), reads=[bb], writes=[b_qT]))
                yield
                proj_fm(4, cc, lambda p, bb, cc=cc: S.op("dve", lambda e: e.tensor_copy(kT[:, cc, t0:t0 + NT], p), reads=[bb], writes=[b_kT]))
                yield
            for j in range(2):
                flush()
                t_, off, bb = next_bank()
                w_, bw = win[5]
                for kc in range(8):
                    S.op("pe", lambda e, t_=t_, off=off, w_=w_, kc=kc, j=j: e.matmul(
                        t_[:, off:off + 512], hn[:, kc, j * 128:(j + 1) * 128], w_[:, kc, :],
                        start=(kc == 0), stop=(kc == 7)), reads=[bw, b_hn], writes=[bb])
                pend.append(lambda t_=t_, off=off, j=j, bb=bb: S.op(
                    "dve", lambda e: e.tensor_copy(vt[:, 2 * i + j, :], t_[:, off:off + 512]), reads=[bb], writes=[b_vt]))
                yield
            flush()
            qkv_ready[i] = True
            yield
            for cc in range(4):
                proj_fm(0, cc, lambda p, bb, cc=cc: S.op("dve", lambda e: e.tensor_copy(ab[:, cc, :], p), reads=[bb], writes=[b_ab]))
                yield
                proj_fm(1, cc, lambda p, bb, cc=cc: S.op("dve", lambda e: e.tensor_copy(u[:, cc, 2:NT + 2], p), reads=[bb], writes=[b_u]))
                yield
                proj_fm(2, cc, lambda p, bb, cc=cc: S.op("dve", lambda e: e.tensor_tensor(
                    u[:, cc, 2:NT + 2], u[:, cc, 2:NT + 2], p, ALU.mult), reads=[bb, b_u], writes=[b_u]))
                yield
            flush()
            for cc in range(4):
                S.op("dve", lambda e, cc=cc: e.tensor_scalar(t1[:, :], u[:, cc, 2:NT + 2], cw[:, 2, cc:cc + 1], None, ALU.mult),
                     reads=[b_u, b_cw], writes=[b_t1])
                S.op("dve", lambda e, cc=cc: e.scalar_tensor_tensor(t1[:, :], u[:, cc, 1:NT + 1], cw[:, 1, cc:cc + 1], t1[:, :],
                                                                    ALU.mult, ALU.add), reads=[b_u, b_cw, b_t1], writes=[b_t1])
                S.op("dve", lambda e, cc=cc: e.scalar_tensor_tensor(t1[:, :], u[:, cc, 0:NT], cw[:, 0, cc:cc + 1], t1[:, :],
                                                                    ALU.mult, ALU.add), reads=[b_u, b_cw, b_t1], writes=[b_t1])
                S.op("dve", lambda e, cc=cc: e.tensor_tensor(yT[:, cc, :], t1[:, :], ab[:, cc, :], ALU.mult),
                     reads=[b_t1, b_ab], writes=[b_ya])
                yield
            S.op("pool", lambda e: e.tensor_copy(u[:, :, 0:2], u[:, :, NT:NT + 2]), reads=[b_u], writes=[b_u])
            done1[i] = True
            yield

        def stage2(i):
            while not qkv_ready[i]:
                yield
            t0 = i * NT
            hT, bh = hTs[i % 2]
            qT, b_qT = qTs[i % 2]
            yT, b_yT = yTs[i % 2]
            b_ya = b_yas[i % 2]
            steps = [(kb, G) for G in (0, 1) for kb in range(2 * i + 1, -1, -1)]
            n = len(steps)
            zt, zb = PP[0]
            ot, ob = PP[3]

            def colof(hl):
                return (hl % 2) * 512 + (hl // 2) * 256

            def do_z(s):
                kb, G = steps[s]
                for hl in range(4):
                    hp = G * 2 + hl // 2
                    pb = (hl % 2) * 64
                    co = colof(hl)
                    S.op("pe", lambda e, hp=hp, pb=pb, co=co, kb=kb: e.matmul(
                        zt[:, co:co + NT], kT[pb:pb + 64, hp, kb * 128:(kb + 1) * 128], qT[pb:pb + 64, hp, :],
                        start=True, stop=True), reads=[b_kTs[kb // 2], b_qT], writes=[zb[hl % 2]])

            def do_esp(s):
                kb, G = steps[s]
                e_, be = eb[s % 3]
                sp_, bsp = spb[s % 3]
                if kb >= 2 * i:
                    m_, bm = (mhi, b_mhi) if kb == 2 * i + 1 else (mlo, b_mlo)
                    for hl in range(4):
                        co = colof(hl)
                        S.op("dve", lambda e, co=co, m_=m_: e.tensor_tensor(e_[:, co:co + NT], zt[:, co:co + NT], m_[:, :], ALU.add),
                             reads=[zb[hl % 2], bm], writes=[be])
                    S.op("act", lambda e: e.activation(e_[:, :], e_[:, :], AF.Exp), reads=[be], writes=[be])
                else:
                    S.op("act", lambda e: e.activation(e_[:, :], zt[:, :], AF.Exp), reads=zb, writes=[be])
                S.op("act", lambda e: e.activation(sp_[:, :], e_[:, :], AF.Ln, bias=onec[:, 0:1]),
                     reads=[be, b_onec], writes=[bsp])

            def do_U(s):
                kb, G = steps[s]
                ct, cb = PP[1]
                sp_, bsp = spb[s % 3]
                first = (kb == 2 * i + 1)
                for hl in range(4):
                    co = colof(hl)
                    S.op("pe", lambda e, co=co, hl=hl: e.matmul(
                        ct[:, co:co + NT], mU[:, :], sp_[:, co:co + NT], start=(first and hl < 2), stop=False,
                        skip_group_check=True), reads=[b_mU, bsp], writes=[cb[hl % 2]])

            def do_g(s):
                kb, G = steps[s]
                ct, cb = PP[1]
                g_, bg = gb[s % 2]
                S.op("act", lambda e: e.activation(g_[:, :], ct[:, :], AF.Exp, scale=-1.0), reads=cb, writes=[bg])

            def do_L(s):
                kb, G = steps[s]
                if kb == 0:
                    return
                ct, cb = PP[1]
                sp_, bsp = spb[s % 3]
                for hl in range(4):
                    co = colof(hl)
                    S.op("pe", lambda e, co=co: e.matmul(
                        ct[:, co:co + NT], mL[:, :], sp_[:, co:co + NT], start=False, stop=False,
                        skip_group_check=True), reads=[b_mL, bsp], writes=[cb[hl % 2]])

            def do_w(s):
                e_, be = eb[s % 3]
                g_, bg = gb[s % 2]
                w_, bw = wb[s % 2]
                S.op("dve", lambda e: e.tensor_tensor(w_[:, :], e_[:, :], g_[:, :], ALU.mult), reads=[be, bg], writes=[bw])

            def do_pv(s):
                kb, G = steps[s]
                w_, bw = wb[s % 2]
                first = (kb == 2 * i + 1)
                for hl in range(4):
                    h = G * 4 + hl
                    pb = (hl % 2) * 64
                    co = colof(hl)
                    oc = G * 512 + (hl // 2) * 256
                    S.op("pe", lambda e, h=h, pb=pb, co=co, oc=oc, kb=kb, hl=hl: e.matmul(
                        ot[pb:pb + 64, oc:oc + NT], vt[:, kb, h * 64:(h + 1) * 64], w_[:, co:co + NT],
                        start=(first and hl < 2), stop=(kb == 0), skip_group_check=True),
                        reads=[b_vts[kb // 2], bw], writes=[ob[G]])

            do_z(0)
            do_esp(0)
            if n > 1:
                do_z(1)
            do_U(0)
            yield
            for j in range(1, n):
                do_esp(j)
                do_g(j - 1)
                if j + 1 < n:
                    do_z(j + 1)
                do_L(j - 1)
                do_U(j)
                do_w(j - 1)
                do_pv(j - 1)
                yield
            do_g(n - 1)
            do_w(n - 1)
            do_pv(n - 1)
            while not done1[i]:
                yield
            for G in range(2):
                S.op("act", lambda e, G=G: e.copy(yT[:, 4 + 2 * G:6 + 2 * G, :],
                                                 ot[:, G * 512:(G + 1) * 512].rearrange("p (k t) -> p k t", k=2)),
                     reads=[ob[G]], writes=[b_yT])
            if "dbg_y" in dr:
                S.dma("sp", lambda e, t0=t0: e.dma_start(out=dr["dbg_y"][:, :, t0:t0 + NT], in_=yT[:, :, :]),
                      reads=[b_yT, b_ya], writes=[b_dst])
            for d in range(8):
                t_, off, bb = zt, (d % 2) * 512, zb[d % 2]
                for cc in range(8):
                    S.op("pe", lambda e, t_=t_, off=off, cc=cc, d=d: e.matmul(
                        t_[:, off:off + NT], wout[:, cc, d * 128:(d + 1) * 128], yT[:, cc, :],
                        start=(cc == 0), stop=(cc == 7)), reads=[b_wout, b_yT, b_ya], writes=[bb])
                S.op("dve", lambda e, t_=t_, off=off, d=d: e.tensor_tensor(hT[:, d, :], hT[:, d, :], t_[:, off:off + NT], ALU.add),
                     reads=[bb, bh], writes=[bh])
                if d % 4 == 3:
                    yield
            S.dma("sp", lambda e, t0=t0: e.dma_start(out=dst[:, :, t0:t0 + NT].rearrange("c p t -> p c t"), in_=hT[:, :, :]),
                  reads=[bh], writes=[b_dst])
            done2[i] = True
            yield
        def chain(mk):
            for i in range(n_tiles):
                yield from mk(i)

        run_zip([chain(stage2), chain(stage1)])
        S.wait_all("sp", [b_dst])
        S.emit()


def hgrn_phase(nc, S, dr, src, b_src, dst, b_dst, ntiles=None):
    NT = 128
    n_tiles = T // NT if ntiles is None else ntiles
    w_in_d = dr["c_w_in"][0]
    w_out_d = dr["c_w_out"][0]
    with ExitStack() as es:
        c = Ctx(nc, S, es)
        (ident, b_ident), (ones, b_ones), (epsb, b_eps) = load_consts(c, S, dr)
        onec, b_onec = c.sb("onec", [128, 1], F32)
        S.op("dve", lambda e: e.memset(onec[:, :], 1.0), writes=[b_onec])
        onesf, b_onesf = c.sb("onesf", [128, 128], F32)
        S.op("dve", lambda e: e.memset(onesf[:, :], 1.0), writes=[b_onesf])
        win = [c.sb("cwin", [128, 8, 512], BF16) for _ in range(8)]
        wout, b_wout = c.sb("cwout", [128, 8, D], BF16)
        gain, b_gain = c.sb("gain", [128, 8], F32)
        onorm, b_onorm = c.sb("onorm", [128, 1], F32)
        mUb, b_mUb = c.sb("mUb", [128, 128], F32)
        mLb, b_mLb = c.sb("mLb", [128, 128], F32)
        lbf, b_lbf = c.sb("lbf", [128, 2, 8], F32)
        omlc, b_omlc = c.sb("omlc", [128, 8], F32)
        omlT, b_omlT = c.sb("omlT", [128, 8, NT], F32)
        lbt, b_lbt = c.sb("lbt", [128, 2, D], F32)
        omlb, b_omlb = c.sb("omlb", [128, D], F32)
        St, b_St = c.sb("St", [128, 8, 128], F32)
        Sbf, b_Sbf = c.sb("Sbf", [128, 8, 128], BF16)
        hT_l = [c.sb("hT", [128, 8, NT], F32) for _ in range(3)]
        sq, b_sq = c.sb("sq", [128, 8, NT], BF16)
        hn, b_hn = c.sb("hn", [128, 8, NT], BF16)
        srt, b_srt = c.sb("srt", [128, NT], F32)
        rstd, b_rstd = c.sb("rstd", [128, NT], F32)
        qs_l = [c.sb("qs", [128, 8, NT], F32) for _ in range(2)]
        kTf_l = [c.sb("kTf", [128, 8, NT], F32) for _ in range(2)]
        gs_l = [c.sb("gs", [128, 8, NT], F32) for _ in range(2)]
        ktok_l = [c.sb("ktok", [128, D], F32) for _ in range(2)]
        logf_l = [c.sb("logf", [128, D], F32) for _ in range(2)]
        itok_l = [c.sb("itok", [128, D], BF16) for _ in range(2)]
        EbT, b_EbT = c.sb("EbT", [128, 8, NT], F32)
        EnbT, b_EnbT = c.sb("EnbT", [128, 8, NT], F32)
        qt, b_qt = c.sb("qt", [128, 8, NT], BF16)
        kt, b_kt = c.sb("kt", [128, 8, NT], BF16)
        Eblr, b_Eblr = c.sb("Eblr", [128, D], F32)
        khat, b_khat = c.sb("khat", [128, D], BF16)
        scm, b_scm = c.sb("scm", [128, 8, NT], BF16)
        osq, b_osq = c.sb("osq", [128, 8, NT], BF16)
        rs, b_rs = c.sb("rs", [128, 8, NT], F32)
        on, b_on = c.sb("on", [128, 8, NT], F32)
        oF_l = [c.sb("oF", [128, 8, NT], BF16)[0] for _ in range(2)]
        HB = {}

        def hb(name, half, par=0):
            k = (name, half, par)
            if k not in HB:
                HB[k] = Buf("%s_%d_%d" % k)
            return HB[k]
        Ctx.N[0] += 1
        pO = es.enter_context(nc.psum_tensor("pO_%d" % Ctx.N[0], [128, 1024], F32))
        b_pO = [Buf("pO0"), Buf("pO1")]
        banks = [c.ps("pb") for _ in range(6)]
        rr = [0]

        rrs = {}

        def next_bank(stream):
            rrs[stream] = rrs.get(stream, 0) + 1
            return banks[2 * stream + rrs[stream] % 2]

        for g in range(8):
            t_, b_ = win[g]
            S.dma("pool", lambda e, t_=t_, g=g: e.dma_start(
                out=t_[:, :, :], in_=w_in_d[:, g * 512:(g + 1) * 512].rearrange("(c p) f -> p c f", p=128)),
                writes=[b_])
        S.dma("pool", lambda e: e.dma_start(out=wout[:, :, :], in_=w_out_d.rearrange("(c p) d -> p c d", p=128)),
              writes=[b_wout])
        S.dma("sp", lambda e: e.dma_start(out=mUb[:, :], in_=dr["c_Ublk"][:, :]), writes=[b_mUb])
        S.dma("sp", lambda e: e.dma_start(out=mLb[:, :], in_=dr["c_Lblk"][:, :]), writes=[b_mLb])
        S.dma("sp", lambda e: e.dma_start(out=gain[:, :], in_=dr["mix_norm"][1].rearrange("(c p) -> p c", p=128),
                                          allow_slow_non_contiguous=True), writes=[b_gain])
        S.dma("sp", lambda e: e.dma_start(out=onorm[:, :], in_=dr["c_out_norm"][0].rearrange("(p o) -> p o", o=1),
                                          allow_slow_non_contiguous=True), writes=[b_onorm])
        S.dma("sp", lambda e: e.dma_start(out=lbf[:, :, :], in_=dr["c_lower_bounds"].rearrange("l (c p) -> p l c", p=128),
                                          allow_slow_non_contiguous=True), writes=[b_lbf])
        for l in range(2):
            S.dma("sp", lambda e, l=l: e.dma_start(out=lbt[:, l, :], in_=dr["c_lower_bounds"][l].partition_broadcast(128)),
                  writes=[b_lbt])
        S.op("dve", lambda e: e.tensor_tensor(omlc[:, :], lbf[:, 0, :], lbf[:, 1, :], ALU.subtract), reads=[b_lbf], writes=[b_omlc])
        S.op("act", lambda e: e.activation(omlc[:, :], omlc[:, :], AF.Sigmoid), reads=[b_omlc], writes=[b_omlc])
        for h in range(8):
            S.op("dve", lambda e, h=h: e.tensor_scalar(omlT[:, h, :], onesf[:, :], omlc[:, h:h + 1], None, ALU.mult),
                 reads=[b_onesf, b_omlc], writes=[b_omlT])
        S.op("dve", lambda e: e.tensor_tensor(omlb[:, :], lbt[:, 0, :], lbt[:, 1, :], ALU.subtract), reads=[b_lbt], writes=[b_omlb])
        S.op("act", lambda e: e.activation(omlb[:, :], omlb[:, :], AF.Sigmoid), reads=[b_omlb], writes=[b_omlb])
        S.op("pool", lambda e: e.memset(St[:, :, :], 0.0), writes=[hb("St", 0), hb("St", 1)])
        S.op("pool", lambda e: e.memset(Sbf[:, :, :], 0.0), writes=[hb("Sbf", 0), hb("Sbf", 1)])

        def bind(i):
            p = i % 2
            return (hT_l[i % 3], qs_l[p], kTf_l[p], gs_l[p], ktok_l[p], logf_l[p], itok_l[p])

        def v3(t_):
            return t_[:, :].rearrange("p (h t) -> p h t", h=4)

        pend = []

        def flush():
            for f in pend:
                f()
            del pend[:]

        def stage1(i):
            t0 = i * NT
            ((hT, bh), (qs, b_qs), (kTf, b_kTf), (gs, b_gs), (ktok, b_ktok), (logf, b_logf), (itok, b_itok)) = bind(i)
            S.dma("sp", lambda e: e.dma_start(out=hT[:, :, :], in_=src[:, :, t0:t0 + NT].rearrange("c p t -> p c t")),
                  reads=[b_src], writes=[bh])
            pss_t, b_pss = next_bank(0)
            norm_tile(S, hT, bh, hn, b_hn, sq, b_sq, pss_t, b_pss, srt, b_srt, rstd, b_rstd,
                      gain, b_gain, ones, b_ones, epsb, b_eps, NT)
            yield

            def proj_fm(G, half, evac):
                flush()
                t_, bb = next_bank(0)
                w_, bw = win[G * 2 + half]
                for hl in range(4):
                    for kc in range(8):
                        S.op("pe", lambda e, hl=hl, kc=kc: e.matmul(
                            t_[:, hl * 128:(hl + 1) * 128], w_[:, kc, hl * 128:(hl + 1) * 128], hn[:, kc, :],
                            start=(kc == 0), stop=(kc == 7)), reads=[bw, b_hn], writes=[bb])
                pend.append(lambda: evac(t_, bb))

            def proj_tm(G, half, evac):
                flush()
                t_, bb = next_bank(0)
                w_, bw = win[G * 2 + half]
                for kc in range(8):
                    S.op("pe", lambda e, kc=kc: e.matmul(t_[:, :], hn[:, kc, :], w_[:, kc, :], start=(kc == 0), stop=(kc == 7)),
                         reads=[bw, b_hn], writes=[bb])
                pend.append(lambda: evac(t_, bb))

            for half in range(2):
                hs = slice(half * 4, half * 4 + 4)
                cs = slice(half * 512, half * 512 + 512)
                bk, bl, bi = hb("ktok", half, i % 2), hb("logf", half, i % 2), hb("itok", half, i % 2)
                bq, bkf, bg = hb("qs", half, i % 2), hb("kTf", half, i % 2), hb("gs", half, i % 2)
                proj_tm(1, half, lambda t_, bb, cs=cs, bk=bk, bl=bl: (
                    S.op("act", lambda e: e.activation(ktok[:, cs], t_[:, :], AF.Exp), reads=[bb], writes=[bk]),
                    S.op("act", lambda e: e.activation(ktok[:, cs], ktok[:, cs], AF.Ln, bias=onec[:, 0:1]), reads=[bk, b_onec], writes=[bk]),
                    S.op("act", lambda e: e.activation(ktok[:, cs], ktok[:, cs], AF.Exp, scale=-1.0), reads=[bk], writes=[bk]),
                    S.op("pool", lambda e: e.tensor_tensor(ktok[:, cs], ktok[:, cs], omlb[:, cs], ALU.mult),
                         reads=[bk, b_omlb], writes=[bk]),
                    S.op("act", lambda e: e.activation(logf[:, cs], ktok[:, cs], AF.Ln, bias=onec[:, 0:1], scale=-1.0),
                         reads=[bk, b_onec], writes=[bl])))
                yield
                proj_tm(2, half, lambda t_, bb, cs=cs, bi=bi: S.op(
                    "dve", lambda e: e.tensor_copy(itok[:, cs], t_[:, :]), reads=[bb], writes=[bi]))
                yield
                proj_fm(0, half, lambda t_, bb, hs=hs, bq=bq: (
                    S.op("act", lambda e: e.activation(qs[:, hs, :], v3(t_), AF.Exp, scale=-1.0), reads=[bb], writes=[bq]),
                    S.op("act", lambda e: e.activation(qs[:, hs, :], qs[:, hs, :], AF.Ln, bias=onec[:, 0:1]), reads=[bq, b_onec], writes=[bq]),
                    S.op("act", lambda e: e.activation(qs[:, hs, :], qs[:, hs, :], AF.Exp, scale=-1.0), reads=[bq], writes=[bq]),
                    S.op("dve", lambda e: e.tensor_tensor(qs[:, hs, :], qs[:, hs, :], v3(t_), ALU.mult), reads=[bq, bb], writes=[bq])))
                yield
                proj_fm(1, half, lambda t_, bb, hs=hs, bkf=bkf: (
                    S.op("act", lambda e: e.activation(kTf[:, hs, :], v3(t_), AF.Exp), reads=[bb], writes=[bkf]),
                    S.op("act", lambda e: e.activation(kTf[:, hs, :], kTf[:, hs, :], AF.Ln, bias=onec[:, 0:1]), reads=[bkf, b_onec], writes=[bkf]),
                    S.op("act", lambda e: e.activation(kTf[:, hs, :], kTf[:, hs, :], AF.Exp, scale=-1.0), reads=[bkf], writes=[bkf]),
                    S.op("pool", lambda e: e.tensor_tensor(kTf[:, hs, :], kTf[:, hs, :], omlT[:, hs, :], ALU.mult),
                         reads=[bkf, b_omlT], writes=[bkf])))
                yield
                proj_fm(3, half, lambda t_, bb, hs=hs, bg=bg: (
                    S.op("act", lambda e: e.activation(gs[:, hs, :], v3(t_), AF.Exp, scale=-1.0), reads=[bb], writes=[bg]),
                    S.op("act", lambda e: e.activation(gs[:, hs, :], gs[:, hs, :], AF.Ln, bias=onec[:, 0:1]), reads=[bg, b_onec], writes=[bg]),
                    S.op("act", lambda e: e.activation(gs[:, hs, :], gs[:, hs, :], AF.Exp, scale=-1.0), reads=[bg], writes=[bg]),
                    S.op("dve", lambda e: e.tensor_tensor(gs[:, hs, :], gs[:, hs, :], v3(t_), ALU.mult), reads=[bg, bb], writes=[bg])))
                yield
            flush()
            yield

        def stage2(i, half):
            ((hT, bh), (qs, _), (kTf, _), (gs, _), (ktok, _), (logf, _), (itok, _)) = bind(i)
            p = i % 2
            bk, bl, bi = hb("ktok", half, p), hb("logf", half, p), hb("itok", half, p)
            bq, bkf, bg = hb("qs", half, p), hb("kTf", half, p), hb("gs", half, p)
            bEb, bEnb, bqt, bkt = hb("EbT", half), hb("EnbT", half), hb("qt", half), hb("kt", half)
            bEbl, bkh, bsc = hb("Eblr", half), hb("khat", half), hb("scm", half)
            bSt, bSbf, bpo = hb("St", half), hb("Sbf", half), b_pO[half]
            bosq, brs, bon, boF = hb("osq", half), hb("rs", half), hb("on", half), hb("oF", half, p)
            oF = oF_l[p]
            hs = slice(half * 4, half * 4 + 4)
            cs = slice(half * 512, half * 512 + 512)
            t_, bb = next_bank(1 + half)
            for hl in range(4):
                h = half * 4 + hl
                S.op("pe", lambda e, hl=hl, h=h: e.matmul(
                    t_[:, hl * 128:(hl + 1) * 128], logf[:, h * 128:(h + 1) * 128], mUb[:, :], start=True, stop=True),
                    reads=[bl, b_mUb], writes=[bb])
            t2, bb2 = next_bank(1 + half)
            for hl in range(4):
                h = half * 4 + hl
                S.op("pe", lambda e, hl=hl, h=h: e.matmul(
                    t2[:, hl * 128:(hl + 1) * 128], mLb[:, :], logf[:, h * 128:(h + 1) * 128], start=True, stop=True),
                    reads=[bl, b_mLb], writes=[bb2])
            yield
            S.op("act", lambda e: e.activation(EbT[:, hs, :], v3(t_), AF.Exp), reads=[bb], writes=[bEb])
            S.op("act", lambda e: e.activation(EnbT[:, hs, :], v3(t_), AF.Exp, scale=-1.0), reads=[bb], writes=[bEnb])
            S.op("act", lambda e: e.activation(Eblr[:, cs], t2[:, :], AF.Exp), reads=[bb2], writes=[bEbl])
            yield
            S.op("dve", lambda e: e.tensor_tensor(qt[:, hs, :], qs[:, hs, :], EbT[:, hs, :], ALU.mult),
                 reads=[bq, bEb], writes=[bqt])
            S.op("dve", lambda e: e.tensor_tensor(kt[:, hs, :], kTf[:, hs, :], EnbT[:, hs, :], ALU.mult),
                 reads=[bkf, bEnb], writes=[bkt])
            S.op("pool", lambda e: e.tensor_tensor(khat[:, cs], ktok[:, cs], Eblr[:, cs], ALU.mult),
                 reads=[bk, bEbl], writes=[bkh])
            yield
            t3, bb3 = next_bank(1 + half)
            for hl in range(4):
                h = half * 4 + hl
                S.op("pe", lambda e, hl=hl, h=h: e.matmul(
                    t3[:, hl * 128:(hl + 1) * 128], kt[:, h, :], qt[:, h, :], start=True, stop=True),
                    reads=[bkt, bqt], writes=[bb3])
            yield
            for hl in range(4):
                h = half * 4 + hl
                S.op("dve", lambda e, hl=hl, h=h: e.tensor_tensor(
                    scm[:, h, :], t3[:, hl * 128:(hl + 1) * 128], mUb[:, :], ALU.mult),
                    reads=[bb3, b_mUb], writes=[bsc])
            yield

            def do_chunk(ch):
                pb = ch * 64
                for hl in range(4):
                    h = half * 4 + hl
                    ocol = half * 512 + hl * 128 + ch * 64
                    S.op("pe", lambda e, h=h, ocol=ocol: e.matmul(
                        pO[:, ocol:ocol + 64], itok[pb:pb + 64, h * 128:(h + 1) * 128], scm[pb:pb + 64, h, pb:pb + 64],
                        start=True, stop=False, skip_group_check=True), reads=[bi, bsc], writes=[bpo])
                    S.op("pe", lambda e, h=h, ocol=ocol: e.matmul(
                        pO[:, ocol:ocol + 64], Sbf[:, h, :], qt[:, h, pb:pb + 64],
                        start=False, stop=True, skip_group_check=True), reads=[bSbf, bqt], writes=[bpo])
                t4, bb4 = next_bank(1 + half)
                for hl in range(4):
                    h = half * 4 + hl
                    S.op("pe", lambda e, hl=hl, h=h: e.matmul(
                        t4[:, hl * 128:(hl + 1) * 128], khat[pb:pb + 64, h * 128:(h + 1) * 128],
                        itok[pb:pb + 64, h * 128:(h + 1) * 128], start=True, stop=True),
                        reads=[bkh, bi], writes=[bb4])
                yield
                for hl in range(4):
                    h = half * 4 + hl
                    S.op("dve", lambda e, hl=hl, h=h: e.scalar_tensor_tensor(
                        St[:, h, :], St[:, h, :], EbT[:, h, pb + 63:pb + 64], t4[:, hl * 128:(hl + 1) * 128],
                        ALU.mult, ALU.add), reads=[bSt, bEb, bb4], writes=[bSt])
                S.op("pool", lambda e: e.tensor_copy(Sbf[:, hs, :], St[:, hs, :]), reads=[bSt], writes=[bSbf])
                yield
            for ch in range(2):
                yield from do_chunk(ch)
            po3 = pO[:, half * 512:(half + 1) * 512].rearrange("p (h t) -> p h t", h=4)
            S.op("act", lambda e: e.activation(osq[:, hs, :], po3, AF.Square), reads=[bpo], writes=[bosq])
            t5, bb5 = next_bank(1 + half)
            for hl in range(4):
                h = half * 4 + hl
                S.op("pe", lambda e, hl=hl, h=h: e.matmul(
                    t5[:, hl * 128:(hl + 1) * 128], ones[:, :], osq[:, h, :], start=True, stop=True),
                    reads=[b_ones, bosq], writes=[bb5])
            yield
            S.op("act", lambda e: e.activation(rs[:, hs, :], v3(t5), AF.Ln, bias=epsb[:, 0:1], scale=1.0 / 128),
                 reads=[bb5, b_eps], writes=[brs])
            S.op("act", lambda e: e.activation(rs[:, hs, :], rs[:, hs, :], AF.Exp, scale=-0.5), reads=[brs], writes=[brs])
            S.op("dve", lambda e: e.tensor_tensor(on[:, hs, :], po3, rs[:, hs, :], ALU.mult), reads=[bpo, brs], writes=[bon])
            S.op("dve", lambda e: e.scalar_tensor_tensor(oF[:, hs, :], on[:, hs, :], onorm[:, 0:1], gs[:, hs, :], ALU.mult, ALU.mult),
                 reads=[bon, b_onorm, bg], writes=[boF])
            yield
            for dh in range(2):
                t6, bb6 = next_bank(1 + half)
                for dl in range(4):
                    d = dh * 4 + dl
                    for cl in range(4):
                        cc = half * 4 + cl
                        S.op("pe", lambda e, dl=dl, d=d, cc=cc, cl=cl, t6=t6: e.matmul(
                            t6[:, dl * 128:(dl + 1) * 128], wout[:, cc, d * 128:(d + 1) * 128], oF[:, cc, :],
                            start=(cl == 0), stop=(cl == 3)), reads=[b_wout, boF], writes=[bb6])
                yield
                S.op("dve", lambda e, dh=dh, t6=t6: e.tensor_tensor(
                    hT[:, dh * 4:dh * 4 + 4, :], hT[:, dh * 4:dh * 4 + 4, :], v3(t6), ALU.add),
                    reads=[bb6, bh], writes=[bh])
                yield

        def store(i):
            t0 = i * NT
            hT, bh = hT_l[i % 3]
            S.dma("sp", lambda e: e.dma_start(out=dst[:, :, t0:t0 + NT].rearrange("c p t -> p c t"), in_=hT[:, :, :]),
                  reads=[bh], writes=[b_dst])

        for _ in stage1(0):
            pass
        for i in range(n_tiles):
            run_zip([stage2(i, 0), stage2(i, 1), stage1(i + 1) if i + 1 < n_tiles else None])
            store(i)
        S.wait_all("sp", [b_dst])
        S.emit()


DRAM_INPUTS = [
    ("x", [T, D]),
    ("ffn_pre_norm", [2, D]), ("ffn_pre_w_gate", [2, D, DFF]), ("ffn_pre_w_up", [2, D, DFF]),
    ("ffn_pre_w_down", [2, DFF, D]), ("mix_norm", [2, D]), ("ffn_post_norm", [2, D]),
    ("ffn_post_w_gate", [2, D, DFF]), ("ffn_post_w_up", [2, D, DFF]), ("ffn_post_w_down", [2, DFF, D]),
    ("ab_w_in", [1, D, 3072]), ("ab_conv_w", [1, 3, 512]), ("ab_w_out", [1, D, D]),
    ("c_w_in", [1, D, 4096]), ("c_lower_bounds", [2, D]), ("c_out_norm", [1, 128]), ("c_w_out", [1, D, D]),
    ("final_norm", [D]),
    ("c_ident", [128, 128]), ("c_U", [128, 128]), ("c_L", [128, 128]), ("c_mlo", [128, 256]), ("c_mhi", [128, 256]),
    ("c_Ublk", [128, 128]), ("c_Lblk", [128, 128]),
]


def host_consts():
    j = np.arange(128)[:, None]
    t = np.arange(128)[None, :]
    tri = (t > j).astype(np.float32)
    return {
        "c_ident": np.eye(128, dtype=np.float32),
        "c_U": (j >= t).astype(np.float32),
        "c_L": (j < t).astype(np.float32),
        "c_mlo": (np.concatenate([tri, np.ones((128, 128), np.float32)], axis=1) - 1.0) * 30000.0,
        "c_mhi": (np.concatenate([np.zeros((128, 128), np.float32), tri], axis=1) - 1.0) * 30000.0,
        "c_Ublk": ((j // 64 == t // 64) & (j <= t)).astype(np.float32),
        "c_Lblk": ((j // 64 == t // 64) & (j > t)).astype(np.float32),
    }


def build_nc(phases=None, dbg=False):
    nc = bass.Bass("TRN2", target_bir_lowering=False)
    dr = {}
    for name, shape in DRAM_INPUTS:
        dr[name] = nc.dram_tensor(name, shape, F32, kind="ExternalInput").ap()
    out = nc.dram_tensor("out", [T, D], F32, kind="ExternalOutput").ap()
    kind = "ExternalOutput" if dbg else "Internal"
    hA = nc.dram_tensor("hA", [8, 128, T], F32, kind=kind).ap()
    hB = nc.dram_tensor("hB", [8, 128, T], F32, kind=kind).ap()
    if dbg:
        dr["dbg_y"] = nc.dram_tensor("dbg_y", [128, 8, T], BF16, kind="ExternalOutput").ap()
    b_x, b_out, b_hA, b_hB = Buf("x"), Buf("out"), Buf("hA"), Buf("hB")
    S = Sched(nc)
    nt = 2 if dbg else None
    if phases is None:
        ffn_phase(nc, S, dr, 0, "pre", dr["x"], b_x, hA, b_hA, first=True)
        ab_phase(nc, S, dr, hA, b_hA, hB, b_hB)
        ffn_phase(nc, S, dr, 0, "post", hB, b_hB, hA, b_hA)
        ffn_phase(nc, S, dr, 1, "pre", hA, b_hA, hB, b_hB)
        hgrn_phase(nc, S, dr, hB, b_hB, hA, b_hA)
        ffn_phase(nc, S, dr, 1, "post", hA, b_hA, out, b_out, last=True)
        return nc
    if "A" in phases:
        ffn_phase(nc, S, dr, 0, "pre", dr["x"], b_x, hA, b_hA, first=True, ntiles=nt)
    if "B" in phases:
        ab_phase(nc, S, dr, hA, b_hA, hB, b_hB, ntiles=nt)
    if "E" in phases:
        hgrn_phase(nc, S, dr, hA, b_hA, hB, b_hB, ntiles=(4 if dbg else None))
    if "F" in phases:
        ffn_phase(nc, S, dr, 1, "post", hA, b_hA, out, b_out, last=True, ntiles=nt)
    return nc


def kernel(**inputs):
    nc = build_nc()
    consts = host_consts()
    in_maps = []
    for b in range(8):
        m = {}
        for name, shape in DRAM_INPUTS:
            if name == "x":
                m[name] = np.ascontiguousarray(inputs["x"][b])
            elif name in consts:
                m[name] = consts[name]
            else:
                m[name] = np.ascontiguousarray(inputs[name], dtype=np.float32)
        in_maps.append(m)
    res = run_bass_kernel_spmd(nc, in_maps, core_ids=list(range(8)))
    return np.stack([r["out"] for r in res.results], axis=0)
```

```python
from contextlib import ExitStack
import numpy as np
import concourse.bass as bass
import concourse.mybir as mybir
from concourse.bass_utils import run_bass_kernel_spmd

F32 = mybir.dt.float32
BF16 = mybir.dt.bfloat16
AF = mybir.ActivationFunctionType
ALU = mybir.AluOpType

D = 1024
T = 4096
DFF = 2816
NFC = DFF // 128
EPS = 1e-6
COMPUTE = ("pe", "act", "dve", "pool")


class Buf:
    __slots__ = ("name", "w", "r")

    def __init__(self, name):
        self.name = name
        self.w = None
        self.r = []


class Sched:
    def __init__(self, nc, n_dma_sems=32, same_engine_sync=True):
        self.nc = nc
        self.eng = {"pe": nc.tensor, "act": nc.scalar, "dve": nc.vector,
                    "pool": nc.gpsimd, "sp": nc.sync}
        self.ops = {e: [] for e in self.eng}
        self.sems = {}
        self.cnt = {}
        for e in COMPUTE:
            self.sems[e] = nc.alloc_semaphore("s_" + e)
            self.cnt[e] = 0
        self.dma_sems = {"sp": [], "pool": []}
        for q, n in (("sp", n_dma_sems), ("pool", 16)):
            for i in range(n):
                k = "%s_d%d" % (q, i)
                self.sems[k] = nc.alloc_semaphore("s_" + k)
                self.cnt[k] = 0
                self.dma_sems[q].append(k)
        self.dma_rr = {"sp": 0, "pool": 0}
        self.seen = {e: {} for e in self.eng}
        self.same_engine_sync = same_engine_sync
        self.n_ops = 0

    def _deps(self, eng, reads, writes, extra=()):
        need = {}

        def add(d):
            if d is None:
                return
            k, v = d
            if k == eng and (eng == "pe" or not self.same_engine_sync):
                return
            if need.get(k, 0) < v:
                need[k] = v
        for b in reads:
            add(b.w)
        for b in writes:
            add(b.w)
            for d in b.r:
                add(d)
        for d in extra:
            add(d)
        seen = self.seen[eng]
        out = []
        for k, v in need.items():
            if seen.get(k, 0) < v:
                seen[k] = v
                out.append((k, v))
        return out

    def _mark(self, me, reads, writes):
        for b in reads:
            b.r.append(me)
            if len(b.r) > 64:
                b.r = b.r[-48:] if False else b.r
        for b in writes:
            b.w = me
            b.r = []

    def op(self, eng, fn, reads=(), writes=()):
        waits = self._deps(eng, reads, writes)
        self.cnt[eng] += 1
        me = (eng, self.cnt[eng])
        self._mark(me, reads, writes)
        self.ops[eng].append((waits, fn, (eng, 1)))
        self.n_ops += 1
        return me

    def dma(self, eng, fn, reads=(), writes=()):
        pool = self.dma_sems[eng]
        k = pool[self.dma_rr[eng] % len(pool)]
        self.dma_rr[eng] += 1
        extra = [(k, self.cnt[k])] if self.cnt[k] else []
        waits = self._deps(eng, reads, writes, extra)
        self.cnt[k] += 16
        me = (k, self.cnt[k])
        self._mark(me, reads, writes)
        self.ops[eng].append((waits, fn, (k, 16)))
        self.n_ops += 1
        return me

    def wait_all(self, eng, bufs):
        waits = self._deps(eng, bufs, bufs)
        self.ops[eng].append((waits, None, None))

    def emit(self):
        nc = self.nc
        sems = self.sems
        ops = self.ops
        with nc.Block() as block:
            def run(engname, e):
                for waits, fn, inc in ops[engname]:
                    for k, v in waits:
                        e.wait_ge(sems[k], v)
                    if fn is not None:
                        fn(e).then_inc(sems[inc[0]], inc[1])

            @block.tensor
            def _(e):
                run("pe", e)

            @block.scalar
            def _(e):
                run("act", e)

            @block.vector
            def _(e):
                run("dve", e)

            @block.gpsimd
            def _(e):
                run("pool", e)

            @block.sync
            def _(e):
                run("sp", e)
        self.ops = {e: [] for e in self.eng}


def run_zip(gens):
    gens = [g for g in gens if g is not None]
    while gens:
        for g in list(gens):
            try:
                next(g)
            except StopIteration:
                gens.remove(g)


class Ctx:
    N = [0]

    def __init__(self, nc, S, es):
        self.nc, self.S, self.es = nc, S, es

    def sb(self, name, shape, dt):
        Ctx.N[0] += 1
        t = self.es.enter_context(self.nc.sbuf_tensor("%s_%d" % (name, Ctx.N[0]), shape, dt))
        return t, Buf(name)

    def ps(self, name):
        Ctx.N[0] += 1
        t = self.es.enter_context(self.nc.psum_tensor("%s_%d" % (name, Ctx.N[0]), [128, 512], F32))
        return t, Buf(name)


def load_consts(c, S, dr):
    ident, b_ident = c.sb("ident", [128, 128], F32)
    ones, b_ones = c.sb("ones", [128, 128], BF16)
    epsb, b_eps = c.sb("epsb", [128, 1], F32)
    S.dma("sp", lambda e: e.dma_start(out=ident[:, :], in_=dr["c_ident"][:, :]), writes=[b_ident])
    S.op("dve", lambda e: e.memset(ones[:, :], 1.0), writes=[b_ones])
    S.op("dve", lambda e: e.memset(epsb[:, :], EPS), writes=[b_eps])
    return (ident, b_ident), (ones, b_ones), (epsb, b_eps)


def ffn_phase(nc, S, dr, layer, which, src, b_src, dst, b_dst, first=False, last=False, ntiles=None):
    NT = 256
    n_tiles = T // NT if ntiles is None else ntiles
    wg_d = dr["ffn_%s_w_gate" % which][layer]
    wu_d = dr["ffn_%s_w_up" % which][layer]
    wd_d = dr["ffn_%s_w_down" % which][layer]
    gain_d = dr["ffn_%s_norm" % which][layer]
    with ExitStack() as es:
        c = Ctx(nc, S, es)
        (ident, b_ident), (ones, b_ones), (epsb, b_eps) = load_consts(c, S, dr)
        wg = [c.sb("wg", [128, 8, DFF // 2], BF16) for _ in range(2)]
        wu = [c.sb("wu", [128, 8, DFF // 2], BF16) for _ in range(2)]
        wd = [c.sb("wd", [128, NFC // 2, D], BF16) for _ in range(2)]
        gain, b_gain = c.sb("gain", [128, 8], F32)
        hT = [c.sb("hT", [128, 8, NT], F32) for _ in range(2)]
        sq, b_sq = c.sb("sq", [128, 8, NT], BF16)
        hn, b_hn = c.sb("hn", [128, 8, NT], BF16)
        srt, b_srt = c.sb("srt", [128, NT], F32)
        rstd, b_rstd = c.sb("rstd", [128, NT], F32)
        act, b_act = c.sb("act", [128, NFC, NT], BF16)
        sg = [c.sb("sg", [128, NT], F32) for _ in range(2)]
        pg = [c.ps("pg") for _ in range(2)]
        pu = [c.ps("pu") for _ in range(2)]
        pd = [c.ps("pd") for _ in range(2)]
        pss, b_pss = c.ps("pss")
        if first or last:
            pT, b_pT = c.ps("pT")
            xs = [c.sb("xs", [128, D], F32) for _ in range(2)]
        if last:
            gfb, b_gfb = c.sb("gfb", [128, D], F32)
            ss1, b_ss1 = c.sb("ss1", [128, 1], F32)
            ss2, b_ss2 = c.sb("ss2", [128, 1], F32)
            junk, b_junk = c.sb("junk", [128, D], BF16)
            S.dma("sp", lambda e: e.dma_start(out=gfb[:, :], in_=dr["final_norm"].partition_broadcast(128)),
                  writes=[b_gfb])

        H = DFF // 2
        for half in range(2):
            for (wt, wsrc) in ((wg, wg_d), (wu, wu_d)):
                t_, b_ = wt[half]
                S.dma("pool", lambda e, t_=t_, wsrc=wsrc, half=half: e.dma_start(
                    out=t_[:, :, :],
                    in_=wsrc[:, half * H:(half + 1) * H].rearrange("(c p) f -> p c f", p=128),
                    max_dma_last_dim=H * 4), writes=[b_])
        for half in range(2):
            t_, b_ = wd[half]
            S.dma("pool", lambda e, t_=t_, half=half: e.dma_start(
                out=t_[:, :, :],
                in_=wd_d[half * (DFF // 2):(half + 1) * (DFF // 2), :].rearrange("(c p) d -> p c d", p=128),
                max_dma_last_dim=4096), writes=[b_])
        S.dma("sp", lambda e: e.dma_start(out=gain[:, :], in_=gain_d.rearrange("(c p) -> p c", p=128),
                                          allow_slow_non_contiguous=True), writes=[b_gain])

        def load_norm(i):
            b = i % 2
            h_, bh = hT[b]
            t0 = i * NT
            if first:
                for j in range(2):
                    x_, bx = xs[j]
                    S.dma("sp", lambda e, x_=x_, j=j: e.dma_start(
                        out=x_[:, :], in_=src[t0 + j * 128: t0 + (j + 1) * 128, :]), reads=[b_src], writes=[bx])
                    for g4 in range(2):
                        for k in range(4):
                            cc = g4 * 4 + k
                            S.op("pe", lambda e, x_=x_, cc=cc, k=k: e.transpose(
                                pT[:, k * 128:(k + 1) * 128], x_[:, cc * 128:(cc + 1) * 128], ident[:, :]),
                                reads=[bx, b_ident], writes=[b_pT])
                        S.op("act", lambda e, h_=h_, g4=g4, j=j: e.copy(
                            h_[:, g4 * 4:(g4 + 1) * 4, j * 128:(j + 1) * 128],
                            pT[:, :].rearrange("p (k t) -> p k t", k=4)), reads=[b_pT], writes=[bh])
            else:
                S.dma("sp", lambda e, h_=h_: e.dma_start(out=h_[:, :, :], in_=src[:, :, t0:t0 + NT].rearrange("c p t -> p c t")),
                      reads=[b_src], writes=[bh])
            S.op("act", lambda e, h_=h_: e.activation(sq[:, :, :], h_[:, :, :], AF.Square), reads=[bh], writes=[b_sq])
            for cc in range(8):
                S.op("pe", lambda e, cc=cc: e.matmul(pss[:, :NT], ones[:, :], sq[:, cc, :], start=(cc == 0), stop=(cc == 7)),
                     reads=[b_ones, b_sq], writes=[b_pss])
            S.op("act", lambda e: e.activation(srt[:, :], pss[:, :NT], AF.Sqrt, bias=epsb[:, 0:1], scale=1.0 / D),
                 reads=[b_pss, b_eps], writes=[b_srt])
            S.op("dve", lambda e: e.reciprocal(rstd[:, :], srt[:, :]), reads=[b_srt], writes=[b_rstd])
            for cc in range(8):
                S.op("dve", lambda e, h_=h_, cc=cc: e.scalar_tensor_tensor(
                    hn[:, cc, :], h_[:, cc, :], gain[:, cc:cc + 1], rstd[:, :], ALU.mult, ALU.mult),
                    reads=[bh, b_gain, b_rstd], writes=[b_hn])

        def gate_up(i):
            for f in range(NFC):
                half, fl = divmod(f, NFC // 2)
                pg_, bpg = pg[f % 2]
                pu_, bpu = pu[f % 2]
                sg_, bsg = sg[f % 2]
                for (wt, p_, bp) in ((wg, pg_, bpg), (wu, pu_, bpu)):
                    w_, bw = wt[half]
                    for cc in range(8):
                        S.op("pe", lambda e, w_=w_, p_=p_, cc=cc, fl=fl: e.matmul(
                            p_[:, :NT], w_[:, cc, fl * 128:(fl + 1) * 128], hn[:, cc, :],
                            start=(cc == 0), stop=(cc == 7)), reads=[bw, b_hn], writes=[bp])
                S.op("act", lambda e, sg_=sg_, pg_=pg_: e.activation(sg_[:, :], pg_[:, :NT], AF.Silu),
                     reads=[bpg], writes=[bsg])
                S.op("dve", lambda e, sg_=sg_, pu_=pu_, f=f: e.tensor_tensor(
                    act[:, f, :], sg_[:, :], pu_[:, :NT], ALU.mult), reads=[bsg, bpu], writes=[b_act])

        def down_store(i):
            b = i % 2
            h_, bh = hT[b]
            t0 = i * NT
            for d in range(8):
                pd_, bpd = pd[d % 2]
                for f in range(NFC):
                    half, fl = divmod(f, NFC // 2)
                    w_, bw = wd[half]
                    S.op("pe", lambda e, w_=w_, pd_=pd_, fl=fl, d=d, f=f: e.matmul(
                        pd_[:, :NT], w_[:, fl, d * 128:(d + 1) * 128], act[:, f, :],
                        start=(f == 0), stop=(f == NFC - 1)), reads=[bw, b_act], writes=[bpd])
                S.op("dve", lambda e, h_=h_, pd_=pd_, d=d: e.scalar_tensor_tensor(
                    h_[:, d, :], pd_[:, :NT], 0.5, h_[:, d, :], ALU.mult, ALU.add),
                    reads=[bpd, bh], writes=[bh])
            if not last:
                S.dma("sp", lambda e, h_=h_: e.dma_start(out=dst[:, :, t0:t0 + NT].rearrange("c p t -> p c t"), in_=h_[:, :, :]),
                      reads=[bh], writes=[b_dst])
            else:
                for j in range(2):
                    x_, bx = xs[j]
                    for g4 in range(2):
                        for k in range(4):
                            cc = g4 * 4 + k
                            S.op("pe", lambda e, h_=h_, cc=cc, k=k, j=j: e.transpose(
                                pT[:, k * 128:(k + 1) * 128], h_[:, cc, j * 128:(j + 1) * 128], ident[:, :]),
                                reads=[bh, b_ident], writes=[b_pT])
                        S.op("act", lambda e, x_=x_, g4=g4: e.copy(x_[:, g4 * 512:(g4 + 1) * 512], pT[:, :]),
                             reads=[b_pT], writes=[bx])
                    S.op("act", lambda e, x_=x_: e.activation(junk[:, :], x_[:, :], AF.Square, accum_out=ss1[:, 0:1]),
                         reads=[bx], writes=[b_junk, b_ss1])
                    S.op("act", lambda e: e.activation(ss2[:, :], ss1[:, :], AF.Sqrt, bias=epsb[:, 0:1], scale=1.0 / D),
                         reads=[b_ss1, b_eps], writes=[b_ss2])
                    S.op("dve", lambda e: e.reciprocal(ss1[:, :], ss2[:, :]), reads=[b_ss2], writes=[b_ss1])
                    S.op("dve", lambda e, x_=x_: e.scalar_tensor_tensor(
                        x_[:, :], x_[:, :], ss1[:, 0:1], gfb[:, :], ALU.mult, ALU.mult),
                        reads=[bx, b_ss1, b_gfb], writes=[bx])
                    S.dma("sp", lambda e, x_=x_, j=j: e.dma_start(
                        out=dst[t0 + j * 128: t0 + (j + 1) * 128, :], in_=x_[:, :]), reads=[bx], writes=[b_dst])

        load_norm(0)
        for i in range(n_tiles):
            gate_up(i)
            if i + 1 < n_tiles:
                load_norm(i + 1)
            down_store(i)
        S.wait_all("sp", [b_dst])
        S.emit()


def norm_tile(S, h_, bh, hn, b_hn, sq, b_sq, pss, b_pss, srt, b_srt, rstd, b_rstd, gain, b_gain,
              ones, b_ones, epsb, b_eps, NT):
    S.op("act", lambda e: e.activation(sq[:, :, :], h_[:, :, :], AF.Square), reads=[bh], writes=[b_sq])
    for cc in range(8):
        S.op("pe", lambda e, cc=cc: e.matmul(pss[:, :NT], ones[:, :], sq[:, cc, :], start=(cc == 0), stop=(cc == 7)),
             reads=[b_ones, b_sq], writes=[b_pss])
    S.op("act", lambda e: e.activation(srt[:, :], pss[:, :NT], AF.Sqrt, bias=epsb[:, 0:1], scale=1.0 / D),
         reads=[b_pss, b_eps], writes=[b_srt])
    S.op("dve", lambda e: e.reciprocal(rstd[:, :], srt[:, :]), reads=[b_srt], writes=[b_rstd])
    for cc in range(8):
        S.op("dve", lambda e, cc=cc: e.scalar_tensor_tensor(
            hn[:, cc, :], h_[:, cc, :], gain[:, cc:cc + 1], rstd[:, :], ALU.mult, ALU.mult),
            reads=[bh, b_gain, b_rstd], writes=[b_hn])


def ab_phase(nc, S, dr, src, b_src, dst, b_dst, ntiles=None):
    NT = 256
    n_tiles = T // NT if ntiles is None else ntiles
    w_in_d = dr["ab_w_in"][0]
    w_out_d = dr["ab_w_out"][0]
    with ExitStack() as es:
        c = Ctx(nc, S, es)
        (ident, b_ident), (ones, b_ones), (epsb, b_eps) = load_consts(c, S, dr)
        onec, b_onec = c.sb("onec", [128, 1], F32)
        S.op("dve", lambda e: e.memset(onec[:, :], 1.0), writes=[b_onec])
        win = [c.sb("win", [128, 8, 512], BF16) for _ in range(6)]
        wout, b_wout = c.sb("wout", [128, 8, D], BF16)
        cw, b_cw = c.sb("cw", [128, 3, 4], F32)
        gain, b_gain = c.sb("gain", [128, 8], F32)
        mU, b_mU = c.sb("mU", [128, 128], BF16)
        mL, b_mL = c.sb("mL", [128, 128], BF16)
        mlo, b_mlo = c.sb("mlo", [128, 256], F32)
        mhi, b_mhi = c.sb("mhi", [128, 256], F32)
        kT, b_kT = c.sb("kT", [128, 4, T], BF16)
        vt, b_vt = c.sb("vt", [128, T // 128, 512], BF16)
        hTs = [c.sb("hT", [128, 8, NT], F32) for _ in range(2)]
        hn, b_hn = c.sb("hn", [128, 8, NT], BF16)
        sq, b_sq = hn, b_hn
        srt, b_srt = c.sb("srt", [128, NT], F32)
        rstd, b_rstd = c.sb("rstd", [128, NT], F32)
        ab, b_ab = c.sb("ab", [128, 4, NT], F32)
        u, b_u = c.sb("u", [128, 4, NT + 2], F32)
        t1, b_t1 = c.sb("t1", [128, NT], F32)
        qTs = [c.sb("qT", [128, 4, NT], BF16) for _ in range(2)]
        yTs = [c.sb("yT", [128, 8, NT], BF16) for _ in range(2)]
        b_kTs = [Buf("kT%d" % i) for i in range(T // NT)]
        b_vts = [Buf("vt%d" % i) for i in range(T // NT)]
        eb = [c.sb("eb", [128, 1024], F32) for _ in range(3)]
        spb = [c.sb("spb", [128, 1024], BF16) for _ in range(3)]
        gb = [c.sb("gb", [128, 1024], F32) for _ in range(2)]
        wb = [c.sb("wb", [128, 1024], BF16) for _ in range(2)]
        PP = []
        for i in range(4):
            Ctx.N[0] += 1
            t_ = es.enter_context(nc.psum_tensor("pp_%d" % Ctx.N[0], [128, 1024], F32))
            PP.append((t_, [Buf("pp%d_0" % i), Buf("pp%d_1" % i)]))
        banks = []
        for b in range(2):
            banks.append((PP[2][0], b * 512, PP[2][1][b]))
        rr = [0]

        def next_bank():
            rr[0] += 1
            return banks[rr[0] % len(banks)]

        for g in range(6):
            t_, b_ = win[g]
            S.dma("pool", lambda e, t_=t_, g=g: e.dma_start(
                out=t_[:, :, :], in_=w_in_d[:, g * 512:(g + 1) * 512].rearrange("(c p) f -> p c f", p=128)),
                writes=[b_])
        S.dma("pool", lambda e: e.dma_start(out=wout[:, :, :], in_=w_out_d.rearrange("(c p) d -> p c d", p=128)),
              writes=[b_wout])
        S.dma("pool", lambda e: e.dma_start(out=mU[:, :], in_=dr["c_U"][:, :]), writes=[b_mU])
        S.dma("pool", lambda e: e.dma_start(out=mL[:, :], in_=dr["c_L"][:, :]), writes=[b_mL])
        S.dma("sp", lambda e: e.dma_start(out=mlo[:, :], in_=dr["c_mlo"][:, :]), writes=[b_mlo])
        S.dma("sp", lambda e: e.dma_start(out=mhi[:, :], in_=dr["c_mhi"][:, :]), writes=[b_mhi])
        S.dma("sp", lambda e: e.dma_start(out=gain[:, :], in_=dr["mix_norm"][0].rearrange("(c p) -> p c", p=128),
                                          allow_slow_non_contiguous=True), writes=[b_gain])
        S.dma("sp", lambda e: e.dma_start(out=cw[:, :, :], in_=dr["ab_conv_w"][0].rearrange("j (c p) -> p j c", p=128),
                                          allow_slow_non_contiguous=True), writes=[b_cw])
        S.op("pool", lambda e: e.memset(u[:, :, 0:2], 0.0), writes=[b_u])

        pend = []

        def flush():
            for f in pend:
                f()
            del pend[:]

        def proj_fm(g, cc, evac):
            flush()
            t_, off, bb = next_bank()
            w_, bw = win[g]
            for kc in range(8):
                S.op("pe", lambda e, t_=t_, off=off, w_=w_, kc=kc, cc=cc: e.matmul(
                    t_[:, off:off + NT], w_[:, kc, cc * 128:(cc + 1) * 128], hn[:, kc, :],
                    start=(kc == 0), stop=(kc == 7)), reads=[bw, b_hn], writes=[bb])
            pend.append(lambda: evac(t_[:, off:off + NT], bb))

        qkv_ready = [False] * n_tiles
        done1 = [False] * n_tiles
        done2 = [False] * n_tiles
        b_yas = [Buf("ya0"), Buf("ya1")]

        def stage1(i):
            while i >= 2 and not done2[i - 2]:
                yield
            t0 = i * NT
            hT, bh = hTs[i % 2]
            qT, b_qT = qTs[i % 2]
            yT, b_yT = yTs[i % 2]
            b_ya = b_yas[i % 2]
            b_kT = b_kTs[i]
            b_vt = b_vts[i]
            S.dma("sp", lambda e, t0=t0: e.dma_start(out=hT[:, :, :], in_=src[:, :, t0:t0 + NT].rearrange("c p t -> p c t")),
                  reads=[b_src], writes=[bh])
            pss_t, pss_off, b_pss = next_bank()
            norm_tile(S, hT, bh, hn, b_hn, sq, b_sq, pss_t[:, pss_off:pss_off + 512], b_pss, srt, b_srt, rstd, b_rstd,
                      gain, b_gain, ones, b_ones, epsb, b_eps, NT)
            yield
            for cc in range(4):
                proj_fm(3, cc, lambda p, bb, cc=cc: S.op("act", lambda e: e.mul(qT[:, cc, :], p, 0.125), reads=[bb], writes=[b_qT]))
                yield
                proj_fm(4, cc, lambda p, bb, cc=cc: S.op("act", lambda e: e.copy(kT[:, cc, t0:t0 + NT], p), reads=[bb], writes=[b_kT]))
                yield
            for j in range(2):
                flush()
                t_, off, bb = next_bank()
                w_, bw = win[5]
                for kc in range(8):
                    S.op("pe", lambda e, t_=t_, off=off, w_=w_, kc=kc, j=j: e.matmul(
                        t_[:, off:off + 512], hn[:, kc, j * 128:(j + 1) * 128], w_[:, kc, :],
                        start=(kc == 0), stop=(kc == 7)), reads=[bw, b_hn], writes=[bb])
                pend.append(lambda t_=t_, off=off, j=j, bb=bb: S.op(
                    "dve", lambda e: e.tensor_copy(vt[:, 2 * i + j, :], t_[:, off:off + 512]), reads=[bb], writes=[b_vt]))
                yield
            flush()
            qkv_ready[i] = True
            yield
            for cc in range(4):
                proj_fm(0, cc, lambda p, bb, cc=cc: S.op("act", lambda e: e.copy(ab[:, cc, :], p), reads=[bb], writes=[b_ab]))
                yield
                proj_fm(1, cc, lambda p, bb, cc=cc: S.op("act", lambda e: e.copy(u[:, cc, 2:NT + 2], p), reads=[bb], writes=[b_u]))
                yield
                proj_fm(2, cc, lambda p, bb, cc=cc: S.op("dve", lambda e: e.tensor_tensor(
                    u[:, cc, 2:NT + 2], u[:, cc, 2:NT + 2], p, ALU.mult), reads=[bb, b_u], writes=[b_u]))
                yield
            flush()
            for cc in range(4):
                S.op("dve", lambda e, cc=cc: e.tensor_scalar(t1[:, :], u[:, cc, 2:NT + 2], cw[:, 2, cc:cc + 1], None, ALU.mult),
                     reads=[b_u, b_cw], writes=[b_t1])
                S.op("dve", lambda e, cc=cc: e.scalar_tensor_tensor(t1[:, :], u[:, cc, 1:NT + 1], cw[:, 1, cc:cc + 1], t1[:, :],
                                                                    ALU.mult, ALU.add), reads=[b_u, b_cw, b_t1], writes=[b_t1])
                S.op("dve", lambda e, cc=cc: e.scalar_tensor_tensor(t1[:, :], u[:, cc, 0:NT], cw[:, 0, cc:cc + 1], t1[:, :],
                                                                    ALU.mult, ALU.add), reads=[b_u, b_cw, b_t1], writes=[b_t1])
                S.op("dve", lambda e, cc=cc: e.tensor_tensor(yT[:, cc, :], t1[:, :], ab[:, cc, :], ALU.mult),
                     reads=[b_t1, b_ab], writes=[b_ya])
                yield
            S.op("pool", lambda e: e.tensor_copy(u[:, :, 0:2], u[:, :, NT:NT + 2]), reads=[b_u], writes=[b_u])
            done1[i] = True
            yield

        def stage2(i):
            while not qkv_ready[i]:
                yield
            t0 = i * NT
            hT, bh = hTs[i % 2]
            qT, b_qT = qTs[i % 2]
            yT, b_yT = yTs[i % 2]
            b_ya = b_yas[i % 2]
            steps = [(kb, G) for G in (0, 1) for kb in range(2 * i + 1, -1, -1)]
            n = len(steps)
            zt, zb = PP[0]
            ot, ob = PP[3]

            def colof(hl):
                return (hl % 2) * 512 + (hl // 2) * 256

            def do_z(s):
                kb, G = steps[s]
                for hl in range(4):
                    hp = G * 2 + hl // 2
                    pb = (hl % 2) * 64
                    co = colof(hl)
                    S.op("pe", lambda e, hp=hp, pb=pb, co=co, kb=kb: e.matmul(
                        zt[:, co:co + NT], kT[pb:pb + 64, hp, kb * 128:(kb + 1) * 128], qT[pb:pb + 64, hp, :],
                        start=True, stop=True), reads=[b_kTs[kb // 2], b_qT], writes=[zb[hl % 2]])

            def do_esp(s):
                kb, G = steps[s]
                e_, be = eb[s % 3]
                sp_, bsp = spb[s % 3]
                if kb >= 2 * i:
                    m_, bm = (mhi, b_mhi) if kb == 2 * i + 1 else (mlo, b_mlo)
                    for hl in range(4):
                        co = colof(hl)
                        S.op("dve", lambda e, co=co, m_=m_: e.tensor_tensor(e_[:, co:co + NT], zt[:, co:co + NT], m_[:, :], ALU.add),
                             reads=[zb[hl % 2], bm], writes=[be])
                    S.op("act", lambda e: e.activation(e_[:, :], e_[:, :], AF.Exp), reads=[be], writes=[be])
                else:
                    S.op("act", lambda e: e.activation(e_[:, :], zt[:, :], AF.Exp), reads=zb, writes=[be])
                S.op("act", lambda e: e.activation(sp_[:, :], e_[:, :], AF.Ln, bias=onec[:, 0:1]),
                     reads=[be, b_onec], writes=[bsp])

            def do_U(s):
                kb, G = steps[s]
                ct, cb = PP[1]
                sp_, bsp = spb[s % 3]
                first = (kb == 2 * i + 1)
                for hl in range(4):
                    co = colof(hl)
                    S.op("pe", lambda e, co=co, hl=hl: e.matmul(
                        ct[:, co:co + NT], mU[:, :], sp_[:, co:co + NT], start=(first and hl < 2), stop=False,
                        skip_group_check=True), reads=[b_mU, bsp], writes=[cb[hl % 2]])

            def do_g(s):
                kb, G = steps[s]
                ct, cb = PP[1]
                g_, bg = gb[s % 2]
                S.op("act", lambda e: e.activation(g_[:, :], ct[:, :], AF.Exp, scale=-1.0), reads=cb, writes=[bg])

            def do_L(s):
                kb, G = steps[s]
                if kb == 0:
                    return
                ct, cb = PP[1]
                sp_, bsp = spb[s % 3]
                for hl in range(4):
                    co = colof(hl)
                    S.op("pe", lambda e, co=co: e.matmul(
                        ct[:, co:co + NT], mL[:, :], sp_[:, co:co + NT], start=False, stop=False,
                        skip_group_check=True), reads=[b_mL, bsp], writes=[cb[hl % 2]])

            def do_w(s):
                e_, be = eb[s % 3]
                g_, bg = gb[s % 2]
                w_, bw = wb[s % 2]
                S.op("dve", lambda e: e.tensor_tensor(w_[:, :], e_[:, :], g_[:, :], ALU.mult), reads=[be, bg], writes=[bw])

            def do_pv(s):
                kb, G = steps[s]
                w_, bw = wb[s % 2]
                first = (kb == 2 * i + 1)
                for hl in range(4):
                    h = G * 4 + hl
                    pb = (hl % 2) * 64
                    co = colof(hl)
                    oc = G * 512 + (hl // 2) * 256
                    S.op("pe", lambda e, h=h, pb=pb, co=co, oc=oc, kb=kb, hl=hl: e.matmul(
                        ot[pb:pb + 64, oc:oc + NT], vt[:, kb, h * 64:(h + 1) * 64], w_[:, co:co + NT],
                        start=(first and hl < 2), stop=(kb == 0), skip_group_check=True),
                        reads=[b_vts[kb // 2], bw], writes=[ob[G]])

            do_z(0)
            do_esp(0)
            if n > 1:
                do_z(1)
            do_U(0)
            yield
            for j in range(1, n):
                do_esp(j)
                do_g(j - 1)
                if j + 1 < n:
                    do_z(j + 1)
                do_L(j - 1)
                do_U(j)
                do_w(j - 1)
                do_pv(j - 1)
                yield
            do_g(n - 1)
            do_w(n - 1)
            do_pv(n - 1)
            while not done1[i]:
                yield
            for G in range(2):
                S.op("act", lambda e, G=G: e.copy(yT[:, 4 + 2 * G:6 + 2 * G, :],
                                                 ot[:, G * 512:(G + 1) * 512].rearrange("p (k t) -> p k t", k=2)),
                     reads=[ob[G]], writes=[b_yT])
            if "dbg_y" in dr:
                S.dma("sp", lambda e, t0=t0: e.dma_start(out=dr["dbg_y"][:, :, t0:t0 + NT], in_=yT[:, :, :]),
                      reads=[b_yT, b_ya], writes=[b_dst])
            for d in range(8):
                t_, off, bb = zt, (d % 2) * 512, zb[d % 2]
                for cc in range(8):
                    S.op("pe", lambda e, t_=t_, off=off, cc=cc, d=d: e.matmul(
                        t_[:, off:off + NT], wout[:, cc, d * 128:(d + 1) * 128], yT[:, cc, :],
                        start=(cc == 0), stop=(cc == 7)), reads=[b_wout, b_yT, b_ya], writes=[bb])
                S.op("dve", lambda e, t_=t_, off=off, d=d: e.tensor_tensor(hT[:, d, :], hT[:, d, :], t_[:, off:off + NT], ALU.add),
                     reads=[bb, bh], writes=[bh])
                if d % 4 == 3:
                    yield
            S.dma("sp", lambda e, t0=t0: e.dma_start(out=dst[:, :, t0:t0 + NT].rearrange("c p t -> p c t"), in_=hT[:, :, :]),
                  reads=[bh], writes=[b_dst])
            done2[i] = True
            yield
        def chain(mk):
            for i in range(n_tiles):
                yield from mk(i)

        run_zip([chain(stage2), chain(stage1)])
        S.wait_all("sp", [b_dst])
        S.emit()


def hgrn_phase(nc, S, dr, src, b_src, dst, b_dst, ntiles=None):
    NT = 128
    n_tiles = T // NT if ntiles is None else ntiles
    w_in_d = dr["c_w_in"][0]
    w_out_d = dr["c_w_out"][0]
    with ExitStack() as es:
        c = Ctx(nc, S, es)
        (ident, b_ident), (ones, b_ones), (epsb, b_eps) = load_consts(c, S, dr)
        onec, b_onec = c.sb("onec", [128, 1], F32)
        S.op("dve", lambda e: e.memset(onec[:, :], 1.0), writes=[b_onec])
        onesf, b_onesf = c.sb("onesf", [128, 128], F32)
        S.op("dve", lambda e: e.memset(onesf[:, :], 1.0), writes=[b_onesf])
        win = [c.sb("cwin", [128, 8, 512], BF16) for _ in range(8)]
        wout, b_wout = c.sb("cwout", [128, 8, D], BF16)
        gain, b_gain = c.sb("gain", [128, 8], F32)
        onorm, b_onorm = c.sb("onorm", [128, 1], F32)
        mUb, b_mUb = c.sb("mUb", [128, 128], F32)
        mLb, b_mLb = c.sb("mLb", [128, 128], F32)
        lbf, b_lbf = c.sb("lbf", [128, 2, 8], F32)
        omlc, b_omlc = c.sb("omlc", [128, 8], F32)
        omlT, b_omlT = c.sb("omlT", [128, 8, NT], F32)
        lbt, b_lbt = c.sb("lbt", [128, 2, D], F32)
        omlb, b_omlb = c.sb("omlb", [128, D], F32)
        St, b_St = c.sb("St", [128, 8, 128], F32)
        Sbf, b_Sbf = c.sb("Sbf", [128, 8, 128], BF16)
        hT_l = [c.sb("hT", [128, 8, NT], F32) for _ in range(3)]
        sq, b_sq = c.sb("sq", [128, 8, NT], BF16)
        hn, b_hn = c.sb("hn", [128, 8, NT], BF16)
        srt, b_srt = c.sb("srt", [128, NT], F32)
        rstd, b_rstd = c.sb("rstd", [128, NT], F32)
        qs_l = [c.sb("qs", [128, 8, NT], F32) for _ in range(2)]
        kTf_l = [c.sb("kTf", [128, 8, NT], F32) for _ in range(2)]
        gs_l = [c.sb("gs", [128, 8, NT], F32) for _ in range(2)]
        ktok_l = [c.sb("ktok", [128, D], F32) for _ in range(2)]
        logf_l = [c.sb("logf", [128, D], F32) for _ in range(2)]
        itok_l = [c.sb("itok", [128, D], BF16) for _ in range(2)]
        EbT, b_EbT = c.sb("EbT", [128, 8, NT], F32)
        EnbT, b_EnbT = c.sb("EnbT", [128, 8, NT], F32)
        qt, b_qt = c.sb("qt", [128, 8, NT], BF16)
        kt, b_kt = c.sb("kt", [128, 8, NT], BF16)
        Eblr, b_Eblr = c.sb("Eblr", [128, D], F32)
        khat, b_khat = c.sb("khat", [128, D], BF16)
        scm, b_scm = c.sb("scm", [128, 8, NT], BF16)
        osq, b_osq = c.sb("osq", [128, 8, NT], BF16)
        rs, b_rs = c.sb("rs", [128, 8, NT], F32)
        on, b_on = c.sb("on", [128, 8, NT], F32)
        oF_l = [c.sb("oF", [128, 8, NT], BF16)[0] for _ in range(2)]
        HB = {}

        def hb(name, half, par=0):
            k = (name, half, par)
            if k not in HB:
                HB[k] = Buf("%s_%d_%d" % k)
            return HB[k]
        Ctx.N[0] += 1
        pO = es.enter_context(nc.psum_tensor("pO_%d" % Ctx.N[0], [128, 1024], F32))
        b_pO = [Buf("pO0"), Buf("pO1")]
        banks = [c.ps("pb") for _ in range(6)]
        rr = [0]

        rrs = {}

        def next_bank(stream):
            rrs[stream] = rrs.get(stream, 0) + 1
            return banks[2 * stream + rrs[stream] % 2]

        for g in range(8):
            t_, b_ = win[g]
            S.dma("pool", lambda e, t_=t_, g=g: e.dma_start(
                out=t_[:, :, :], in_=w_in_d[:, g * 512:(g + 1) * 512].rearrange("(c p) f -> p c f", p=128)),
                writes=[b_])
        S.dma("pool", lambda e: e.dma_start(out=wout[:, :, :], in_=w_out_d.rearrange("(c p) d -> p c d", p=128)),
              writes=[b_wout])
        S.dma("sp", lambda e: e.dma_start(out=mUb[:, :], in_=dr["c_Ublk"][:, :]), writes=[b_mUb])
        S.dma("sp", lambda e: e.dma_start(out=mLb[:, :], in_=dr["c_Lblk"][:, :]), writes=[b_mLb])
        S.dma("sp", lambda e: e.dma_start(out=gain[:, :], in_=dr["mix_norm"][1].rearrange("(c p) -> p c", p=128),
                                          allow_slow_non_contiguous=True), writes=[b_gain])
        S.dma("sp", lambda e: e.dma_start(out=onorm[:, :], in_=dr["c_out_norm"][0].rearrange("(p o) -> p o", o=1),
                                          allow_slow_non_contiguous=True), writes=[b_onorm])
        S.dma("sp", lambda e: e.dma_start(out=lbf[:, :, :], in_=dr["c_lower_bounds"].rearrange("l (c p) -> p l c", p=128),
                                          allow_slow_non_contiguous=True), writes=[b_lbf])
        for l in range(2):
            S.dma("sp", lambda e, l=l: e.dma_start(out=lbt[:, l, :], in_=dr["c_lower_bounds"][l].partition_broadcast(128)),
                  writes=[b_lbt])
        S.op("dve", lambda e: e.tensor_tensor(omlc[:, :], lbf[:, 0, :], lbf[:, 1, :], ALU.subtract), reads=[b_lbf], writes=[b_omlc])
        S.op("act", lambda e: e.activation(omlc[:, :], omlc[:, :], AF.Sigmoid), reads=[b_omlc], writes=[b_omlc])
        for h in range(8):
            S.op("dve", lambda e, h=h: e.tensor_scalar(omlT[:, h, :], onesf[:, :], omlc[:, h:h + 1], None, ALU.mult),
                 reads=[b_onesf, b_omlc], writes=[b_omlT])
        S.op("dve", lambda e: e.tensor_tensor(omlb[:, :], lbt[:, 0, :], lbt[:, 1, :], ALU.subtract), reads=[b_lbt], writes=[b_omlb])
        S.op("act", lambda e: e.activation(omlb[:, :], omlb[:, :], AF.Sigmoid), reads=[b_omlb], writes=[b_omlb])
        S.op("pool", lambda e: e.memset(St[:, :, :], 0.0), writes=[hb("St", 0), hb("St", 1)])
        S.op("pool", lambda e: e.memset(Sbf[:, :, :], 0.0), writes=[hb("Sbf", 0), hb("Sbf", 1)])

        def bind(i):
            p = i % 2
            return (hT_l[i % 3], qs_l[p], kTf_l[p], gs_l[p], ktok_l[p], logf_l[p], itok_l[p])

        def v3(t_):
            return t_[:, :].rearrange("p (h t) -> p h t", h=4)

        pend = []

        def flush():
            for f in pend:
                f()
            del pend[:]

        def stage1(i):
            t0 = i * NT
            ((hT, bh), (qs, b_qs), (kTf, b_kTf), (gs, b_gs), (ktok, b_ktok), (logf, b_logf), (itok, b_itok)) = bind(i)
            S.dma("sp", lambda e: e.dma_start(out=hT[:, :, :], in_=src[:, :, t0:t0 + NT].rearrange("c p t -> p c t")),
                  reads=[b_src], writes=[bh])
            pss_t, b_pss = next_bank(0)
            norm_tile(S, hT, bh, hn, b_hn, sq, b_sq, pss_t, b_pss, srt, b_srt, rstd, b_rstd,
                      gain, b_gain, ones, b_ones, epsb, b_eps, NT)
            yield

            def proj_fm(G, half, evac):
                flush()
                t_, bb = next_bank(0)
                w_, bw = win[G * 2 + half]
                for hl in range(4):
                    for kc in range(8):
                        S.op("pe", lambda e, hl=hl, kc=kc: e.matmul(
                            t_[:, hl * 128:(hl + 1) * 128], w_[:, kc, hl * 128:(hl + 1) * 128], hn[:, kc, :],
                            start=(kc == 0), stop=(kc == 7)), reads=[bw, b_hn], writes=[bb])
                pend.append(lambda: evac(t_, bb))

            def proj_tm(G, half, evac):
                flush()
                t_, bb = next_bank(0)
                w_, bw = win[G * 2 + half]
                for kc in range(8):
                    S.op("pe", lambda e, kc=kc: e.matmul(t_[:, :], hn[:, kc, :], w_[:, kc, :], start=(kc == 0), stop=(kc == 7)),
                         reads=[bw, b_hn], writes=[bb])
                pend.append(lambda: evac(t_, bb))

            for half in range(2):
                hs = slice(half * 4, half * 4 + 4)
                cs = slice(half * 512, half * 512 + 512)
                bk, bl, bi = hb("ktok", half, i % 2), hb("logf", half, i % 2), hb("itok", half, i % 2)
                bq, bkf, bg = hb("qs", half, i % 2), hb("kTf", half, i % 2), hb("gs", half, i % 2)
                proj_tm(1, half, lambda t_, bb, cs=cs, bk=bk, bl=bl: (
                    S.op("act", lambda e: e.activation(ktok[:, cs], t_[:, :], AF.Exp), reads=[bb], writes=[bk]),
                    S.op("act", lambda e: e.activation(ktok[:, cs], ktok[:, cs], AF.Ln, bias=onec[:, 0:1]), reads=[bk, b_onec], writes=[bk]),
                    S.op("act", lambda e: e.activation(ktok[:, cs], ktok[:, cs], AF.Exp, scale=-1.0), reads=[bk], writes=[bk]),
                    S.op("pool", lambda e: e.tensor_tensor(ktok[:, cs], ktok[:, cs], omlb[:, cs], ALU.mult),
                         reads=[bk, b_omlb], writes=[bk]),
                    S.op("act", lambda e: e.activation(logf[:, cs], ktok[:, cs], AF.Ln, bias=onec[:, 0:1], scale=-1.0),
                         reads=[bk, b_onec], writes=[bl])))
                yield
                proj_tm(2, half, lambda t_, bb, cs=cs, bi=bi: S.op(
                    "dve", lambda e: e.tensor_copy(itok[:, cs], t_[:, :]), reads=[bb], writes=[bi]))
                yield
                proj_fm(0, half, lambda t_, bb, hs=hs, bq=bq: (
                    S.op("act", lambda e: e.activation(qs[:, hs, :], v3(t_), AF.Exp, scale=-1.0), reads=[bb], writes=[bq]),
                    S.op("act", lambda e: e.activation(qs[:, hs, :], qs[:, hs, :], AF.Ln, bias=onec[:, 0:1]), reads=[bq, b_onec], writes=[bq]),
                    S.op("act", lambda e: e.activation(qs[:, hs, :], qs[:, hs, :], AF.Exp, scale=-1.0), reads=[bq], writes=[bq]),
                    S.op("dve", lambda e: e.tensor_tensor(qs[:, hs, :], qs[:, hs, :], v3(t_), ALU.mult), reads=[bq, bb], writes=[bq])))
                yield
                proj_fm(1, half, lambda t_, bb, hs=hs, bkf=bkf: (
                    S.op("act", lambda e: e.activation(kTf[:, hs, :], v3(t_), AF.Exp), reads=[bb], writes=[bkf]),
                    S.op("act", lambda e: e.activation(kTf[:, hs, :], kTf[:, hs, :], AF.Ln, bias=onec[:, 0:1]), reads=[bkf, b_onec], writes=[bkf]),
                    S.op("act", lambda e: e.activation(kTf[:, hs, :], kTf[:, hs, :], AF.Exp, scale=-1.0), reads=[bkf], writes=[bkf]),
                    S.op("pool", lambda e: e.tensor_tensor(kTf[:, hs, :], kTf[:, hs, :], omlT[:, hs, :], ALU.mult),
                         reads=[bkf, b_omlT], writes=[bkf])))
                yield
                proj_fm(3, half, lambda t_, bb, hs=hs, bg=bg: (
                    S.op("act", lambda e: e.activation(gs[:, hs, :], v3(t_), AF.Exp, scale=-1.0), reads=[bb], writes=[bg]),
                    S.op("act", lambda e: e.activation(gs[:, hs, :], gs[:, hs, :], AF.Ln, bias=onec[:, 0:1]), reads=[bg, b_onec], writes=[bg]),
                    S.op("act", lambda e: e.activation(gs[:, hs, :], gs[:, hs, :], AF.Exp, scale=-1.0), reads=[bg], writes=[bg]),
                    S.op("dve", lambda e: e.tensor_tensor(gs[:, hs, :], gs[:, hs, :], v3(t_), ALU.mult), reads=[bg, bb], writes=[bg])))
                yield
            flush()
            yield

        def stage2(i, half):
            ((hT, bh), (qs, _), (kTf, _), (gs, _), (ktok, _), (logf, _), (itok, _)) = bind(i)
            p = i % 2
            bk, bl, bi = hb("ktok", half, p), hb("logf", half, p), hb("itok", half, p)
            bq, bkf, bg = hb("qs", half, p), hb("kTf", half, p), hb("gs", half, p)
            bEb, bEnb, bqt, bkt = hb("EbT", half), hb("EnbT", half), hb("qt", half), hb("kt", half)
            bEbl, bkh, bsc = hb("Eblr", half), hb("khat", half), hb("scm", half)
            bSt, bSbf, bpo = hb("St", half), hb("Sbf", half), b_pO[half]
            bosq, brs, bon, boF = hb("osq", half), hb("rs", half), hb("on", half), hb("oF", half, p)
            oF = oF_l[p]
            hs = slice(half * 4, half * 4 + 4)
            cs = slice(half * 512, half * 512 + 512)
            t_, bb = next_bank(1 + half)
            for hl in range(4):
                h = half * 4 + hl
                S.op("pe", lambda e, hl=hl, h=h: e.matmul(
                    t_[:, hl * 128:(hl + 1) * 128], logf[:, h * 128:(h + 1) * 128], mUb[:, :], start=True, stop=True),
                    reads=[bl, b_mUb], writes=[bb])
            t2, bb2 = next_bank(1 + half)
            for hl in range(4):
                h = half * 4 + hl
                S.op("pe", lambda e, hl=hl, h=h: e.matmul(
                    t2[:, hl * 128:(hl + 1) * 128], mLb[:, :], logf[:, h * 128:(h + 1) * 128], start=True, stop=True),
                    reads=[bl, b_mLb], writes=[bb2])
            yield
            S.op("act", lambda e: e.activation(EbT[:, hs, :], v3(t_), AF.Exp), reads=[bb], writes=[bEb])
            S.op("act", lambda e: e.activation(EnbT[:, hs, :], v3(t_), AF.Exp, scale=-1.0), reads=[bb], writes=[bEnb])
            S.op("act", lambda e: e.activation(Eblr[:, cs], t2[:, :], AF.Exp), reads=[bb2], writes=[bEbl])
            yield
            S.op("dve", lambda e: e.tensor_tensor(qt[:, hs, :], qs[:, hs, :], EbT[:, hs, :], ALU.mult),
                 reads=[bq, bEb], writes=[bqt])
            S.op("dve", lambda e: e.tensor_tensor(kt[:, hs, :], kTf[:, hs, :], EnbT[:, hs, :], ALU.mult),
                 reads=[bkf, bEnb], writes=[bkt])
            S.op("pool", lambda e: e.tensor_tensor(khat[:, cs], ktok[:, cs], Eblr[:, cs], ALU.mult),
                 reads=[bk, bEbl], writes=[bkh])
            yield
            t3, bb3 = next_bank(1 + half)
            for hl in range(4):
                h = half * 4 + hl
                S.op("pe", lambda e, hl=hl, h=h: e.matmul(
                    t3[:, hl * 128:(hl + 1) * 128], kt[:, h, :], qt[:, h, :], start=True, stop=True),
                    reads=[bkt, bqt], writes=[bb3])
            yield
            for hl in range(4):
                h = half * 4 + hl
                S.op("dve", lambda e, hl=hl, h=h: e.tensor_tensor(
                    scm[:, h, :], t3[:, hl * 128:(hl + 1) * 128], mUb[:, :], ALU.mult),
                    reads=[bb3, b_mUb], writes=[bsc])
            yield

            def do_chunk(ch):
                pb = ch * 64
                for hl in range(4):
                    h = half * 4 + hl
                    ocol = half * 512 + hl * 128 + ch * 64
                    S.op("pe", lambda e, h=h, ocol=ocol: e.matmul(
                        pO[:, ocol:ocol + 64], itok[pb:pb + 64, h * 128:(h + 1) * 128], scm[pb:pb + 64, h, pb:pb + 64],
                        start=True, stop=False, skip_group_check=True), reads=[bi, bsc], writes=[bpo])
                    S.op("pe", lambda e, h=h, ocol=ocol: e.matmul(
                        pO[:, ocol:ocol + 64], Sbf[:, h, :], qt[:, h, pb:pb + 64],
                        start=False, stop=True, skip_group_check=True), reads=[bSbf, bqt], writes=[bpo])
                t4, bb4 = next_bank(1 + half)
                for hl in range(4):
                    h = half * 4 + hl
                    S.op("pe", lambda e, hl=hl, h=h: e.matmul(
                        t4[:, hl * 128:(hl + 1) * 128], khat[pb:pb + 64, h * 128:(h + 1) * 128],
                        itok[pb:pb + 64, h * 128:(h + 1) * 128], start=True, stop=True),
                        reads=[bkh, bi], writes=[bb4])
                yield
                for hl in range(4):
                    h = half * 4 + hl
                    S.op("dve", lambda e, hl=hl, h=h: e.scalar_tensor_tensor(
                        St[:, h, :], St[:, h, :], EbT[:, h, pb + 63:pb + 64], t4[:, hl * 128:(hl + 1) * 128],
                        ALU.mult, ALU.add), reads=[bSt, bEb, bb4], writes=[bSt])
                S.op("pool", lambda e: e.tensor_copy(Sbf[:, hs, :], St[:, hs, :]), reads=[bSt], writes=[bSbf])
                yield
            for ch in range(2):
                yield from do_chunk(ch)
            po3 = pO[:, half * 512:(half + 1) * 512].rearrange("p (h t) -> p h t", h=4)
            S.op("act", lambda e: e.activation(osq[:, hs, :], po3, AF.Square), reads=[bpo], writes=[bosq])
            t5, bb5 = next_bank(1 + half)
            for hl in range(4):
                h = half * 4 + hl
                S.op("pe", lambda e, hl=hl, h=h: e.matmul(
                    t5[:, hl * 128:(hl + 1) * 128], ones[:, :], osq[:, h, :], start=True, stop=True),
                    reads=[b_ones, bosq], writes=[bb5])
            yield
            S.op("act", lambda e: e.activation(rs[:, hs, :], v3(t5), AF.Ln, bias=epsb[:, 0:1], scale=1.0 / 128),
                 reads=[bb5, b_eps], writes=[brs])
            S.op("act", lambda e: e.activation(rs[:, hs, :], rs[:, hs, :], AF.Exp, scale=-0.5), reads=[brs], writes=[brs])
            S.op("dve", lambda e: e.tensor_tensor(on[:, hs, :], po3, rs[:, hs, :], ALU.mult), reads=[bpo, brs], writes=[bon])
            S.op("dve", lambda e: e.scalar_tensor_tensor(oF[:, hs, :], on[:, hs, :], onorm[:, 0:1], gs[:, hs, :], ALU.mult, ALU.mult),
                 reads=[bon, b_onorm, bg], writes=[boF])
            yield
            for dh in range(2):
                t6, bb6 = next_bank(1 + half)
                for dl in range(4):
                    d = dh * 4 + dl
                    for cl in range(4):
                        cc = half * 4 + cl
                        S.op("pe", lambda e, dl=dl, d=d, cc=cc, cl=cl, t6=t6: e.matmul(
                            t6[:, dl * 128:(dl + 1) * 128], wout[:, cc, d * 128:(d + 1) * 128], oF[:, cc, :],
                            start=(cl == 0), stop=(cl == 3)), reads=[b_wout, boF], writes=[bb6])
                yield
                S.op("dve", lambda e, dh=dh, t6=t6: e.tensor_tensor(
                    hT[:, dh * 4:dh * 4 + 4, :], hT[:, dh * 4:dh * 4 + 4, :], v3(t6), ALU.add),
                    reads=[bb6, bh], writes=[bh])
                yield

        def store(i):
            t0 = i * NT
            hT, bh = hT_l[i % 3]
            S.dma("sp", lambda e: e.dma_start(out=dst[:, :, t0:t0 + NT].rearrange("c p t -> p c t"), in_=hT[:, :, :]),
                  reads=[bh], writes=[b_dst])

        for _ in stage1(0):
            pass
        for i in range(n_tiles):
            run_zip([stage2(i, 0), stage2(i, 1), stage1(i + 1) if i + 1 < n_tiles else None])
            store(i)
        S.wait_all("sp", [b_dst])
        S.emit()


DRAM_INPUTS = [
    ("x", [T, D]),
    ("ffn_pre_norm", [2, D]), ("ffn_pre_w_gate", [2, D, DFF]), ("ffn_pre_w_up", [2, D, DFF]),
    ("ffn_pre_w_down", [2, DFF, D]), ("mix_norm", [2, D]), ("ffn_post_norm", [2, D]),
    ("ffn_post_w_gate", [2, D, DFF]), ("ffn_post_w_up", [2, D, DFF]), ("ffn_post_w_down", [2, DFF, D]),
    ("ab_w_in", [1, D, 3072]), ("ab_conv_w", [1, 3, 512]), ("ab_w_out", [1, D, D]),
    ("c_w_in", [1, D, 4096]), ("c_lower_bounds", [2, D]), ("c_out_norm", [1, 128]), ("c_w_out", [1, D, D]),
    ("final_norm", [D]),
    ("c_ident", [128, 128]), ("c_U", [128, 128]), ("c_L", [128, 128]), ("c_mlo", [128, 256]), ("c_mhi", [128, 256]),
    ("c_Ublk", [128, 128]), ("c_Lblk", [128, 128]),
]


def host_consts():
    j = np.arange(128)[:, None]
    t = np.arange(128)[None, :]
    tri = (t > j).astype(np.float32)
    return {
        "c_ident": np.eye(128, dtype=np.float32),
        "c_U": (j >= t).astype(np.float32),
        "c_L": (j < t).astype(np.float32),
        "c_mlo": (np.concatenate([tri, np.ones((128, 128), np.float32)], axis=1) - 1.0) * 30000.0,
        "c_mhi": (np.concatenate([np.zeros((128, 128), np.float32), tri], axis=1) - 1.0) * 30000.0,
        "c_Ublk": ((j // 64 == t // 64) & (j <= t)).astype(np.float32),
        "c_Lblk": ((j // 64 == t // 64) & (j > t)).astype(np.float32),
    }


def build_nc(phases=None, dbg=False):
    nc = bass.Bass("TRN2", target_bir_lowering=False)
    dr = {}
    for name, shape in DRAM_INPUTS:
        dr[name] = nc.dram_tensor(name, shape, F32, kind="ExternalInput").ap()
    out = nc.dram_tensor("out", [T, D], F32, kind="ExternalOutput").ap()
    kind = "ExternalOutput" if dbg else "Internal"
    hA = nc.dram_tensor("hA", [8, 128, T], F32, kind=kind).ap()
    hB = nc.dram_tensor("hB", [8, 128, T], F32, kind=kind).ap()
    if dbg:
        dr["dbg_y"] = nc.dram_tensor("dbg_y", [128, 8, T], BF16, kind="ExternalOutput").ap()
    b_x, b_out, b_hA, b_hB = Buf("x"), Buf("out"), Buf("hA"), Buf("hB")
    S = Sched(nc)
    nt = 2 if dbg else None
    if phases is None:
        ffn_phase(nc, S, dr, 0, "pre", dr["x"], b_x, hA, b_hA, first=True)
        ab_phase(nc, S, dr, hA, b_hA, hB, b_hB)
        ffn_phase(nc, S, dr, 0, "post", hB, b_hB, hA, b_hA)
        ffn_phase(nc, S, dr, 1, "pre", hA, b_hA, hB, b_hB)
        hgrn_phase(nc, S, dr, hB, b_hB, hA, b_hA)
        ffn_phase(nc, S, dr, 1, "post", hA, b_hA, out, b_out, last=True)
        return nc
    if "A" in phases:
        ffn_phase(nc, S, dr, 0, "pre", dr["x"], b_x, hA, b_hA, first=True, ntiles=nt)
    if "B" in phases:
        ab_phase(nc, S, dr, hA, b_hA, hB, b_hB, ntiles=nt)
    if "E" in phases:
        hgrn_phase(nc, S, dr, hA, b_hA, hB, b_hB, ntiles=(4 if dbg else None))
    if "F" in phases:
        ffn_phase(nc, S, dr, 1, "post", hA, b_hA, out, b_out, last=True, ntiles=nt)
    return nc


def kernel(**inputs):
    nc = build_nc()
    consts = host_consts()
    in_maps = []
    for b in range(8):
        m = {}
        for name, shape in DRAM_INPUTS:
            if name == "x":
                m[name] = np.ascontiguousarray(inputs["x"][b])
            elif name in consts:
                m[name] = consts[name]
            else:
                m[name] = np.ascontiguousarray(inputs[name], dtype=np.float32)
        in_maps.append(m)
    res = run_bass_kernel_spmd(nc, in_maps, core_ids=list(range(8)))
    return np.stack([r["out"] for r in res.results], axis=0)
```
